# Optimizing a Trainium2 kernel written in Bass

```python
import jax, jax.numpy as jnp
from jax import lax
import numpy as np

D_MODEL = 1024
BATCH = 1
SEQ = 16384
DEPTH = 1

CHUNK = 64
HEAD_DIM = 64
RWKV_HEADS = 8
D_RWKV = RWKV_HEADS * HEAD_DIM
DECAY_LORA = 64
ICLR_LORA = 64
DECAY_SCALE = 0.606531
GN_EPS = 64e-5
ATT_HEADS = 8
D_ATT = ATT_HEADS * HEAD_DIM
IDX_HEADS = 8
IDX_DIM = 64
TOPK_MAX = 256
Q_BLOCK = 128
NORM_EPS = 1e-6

N_RWKV_COLS = 3 * D_RWKV + DECAY_LORA + ICLR_LORA + D_RWKV
N_DSA_COLS = 3 * D_ATT + D_ATT + IDX_HEADS * IDX_DIM + IDX_DIM + IDX_HEADS
N_MERGE_COLS = 2 * D_MODEL
N_IN = N_RWKV_COLS + N_DSA_COLS + N_MERGE_COLS
RWKV_SPLITS = (D_RWKV, 2 * D_RWKV, 3 * D_RWKV, 3 * D_RWKV + DECAY_LORA,
               3 * D_RWKV + DECAY_LORA + ICLR_LORA)
DSA_SPLITS = (D_ATT, 2 * D_ATT, 3 * D_ATT, 4 * D_ATT,
              4 * D_ATT + IDX_HEADS * IDX_DIM,
              4 * D_ATT + IDX_HEADS * IDX_DIM + IDX_DIM)

kernel_name = "hybrid_rwkv7_dsa_gated_block"


def rms_norm(x, g, eps=NORM_EPS):
    xf = x.astype(jnp.float32)
    y = xf * lax.rsqrt(jnp.mean(xf * xf, axis=-1, keepdims=True) + eps)
    return (y * g.astype(jnp.float32)).astype(x.dtype)


def rwkv7_time_mix(pa, mu, w0, w_up, a0, a_up, k_k, k_a, r_k, gn_w, gn_b):
    B, S, _ = pa.shape
    prev = jnp.pad(pa, ((0, 0), (1, 0), (0, 0)))[:, :S]
    pa = pa + mu * (prev - pa)
    r, k, v, wd, ad, g = jnp.split(pa, RWKV_SPLITS, axis=-1)
    w = jnp.exp(-DECAY_SCALE * jax.nn.sigmoid((w0 + jnp.tanh(wd) @ w_up).astype(jnp.float32)))
    a = jax.nn.sigmoid((a0 + ad @ a_up).astype(jnp.float32))
    kk = (k * k_k).astype(jnp.float32).reshape(B, S, RWKV_HEADS, HEAD_DIM)
    kk = kk / jnp.maximum(jnp.linalg.norm(kk, axis=-1, keepdims=True), 1e-12)
    k = (k * (1.0 + (a - 1.0) * k_a)).astype(jnp.float32)
    hs = lambda t: t.astype(jnp.float32).reshape(B, S, RWKV_HEADS, HEAD_DIM)
    r_h, w_h, k_h, v_h, a_h = hs(r), hs(w), hs(k), hs(v), hs(a)
    b_h = kk * a_h
    tm = lambda t: jnp.moveaxis(t, 1, 0)

    def step(state, inp):
        r_t, w_t, k_t, v_t, nkk_t, b_t = inp
        sa = jnp.einsum('bhvk,bhk->bhv', state, nkk_t)
        state = (state * w_t[:, :, None, :]
                 + sa[..., None] * b_t[:, :, None, :]
                 + v_t[..., None] * k_t[:, :, None, :])
        y_t = jnp.einsum('bhvk,bhk->bhv', state, r_t)
        return state, y_t

    s0 = jnp.zeros((B, RWKV_HEADS, HEAD_DIM, HEAD_DIM), jnp.float32)
    _, y = lax.scan(step, s0, (tm(r_h), tm(w_h), tm(k_h), tm(v_h), tm(-kk), tm(b_h)))
    y = jnp.moveaxis(y, 0, 1)
    mean = jnp.mean(y, axis=-1, keepdims=True)
    var = jnp.mean(jnp.square(y - mean), axis=-1, keepdims=True)
    y = (y - mean) * lax.rsqrt(var + GN_EPS)
    y = y * gn_w.reshape(RWKV_HEADS, HEAD_DIM) + gn_b.reshape(RWKV_HEADS, HEAD_DIM)
    bonus = jnp.sum(r_h * k_h * r_k.astype(jnp.float32), axis=-1, keepdims=True) * v_h
    y = (y + bonus).reshape(B, S, D_RWKV).astype(pa.dtype)
    return y, g


def dsa_sparse_attention(q, k, v, iq, ik, iw):
    B, S, H, Dh = q.shape
    topk = min(TOPK_MAX, S // 4)
    nb = S // Q_BLOCK
    key_chunk = jnp.arange(S) // CHUNK
    ik32 = ik.astype(jnp.float32)
    iw32 = iw.astype(jnp.float32) * (IDX_HEADS ** -0.5)
    blocks = lambda t: jnp.moveaxis(t.reshape((B, nb, Q_BLOCK) + t.shape[2:]), 1, 0)
    q_pos = jnp.arange(S).reshape(nb, Q_BLOCK)
    scale = HEAD_DIM ** -0.5

    def block_fn(args):
        qb, iqb, iwb, pos = args
        q_chunk = pos // CHUNK
        sc = jnp.einsum('bqhd,bsd->bqhs', iqb.astype(jnp.float32), ik32) * (IDX_DIM ** -0.5)
        idx_score = jnp.einsum('bqhs,bqh->bqs', jax.nn.relu(sc), iwb)
        adm = key_chunk[None, :] <= q_chunk[:, None]
        idx_score = jnp.where(adm[None], idx_score, -jnp.inf)
        _, sel = lax.top_k(idx_score, topk)
        valid = (sel // CHUNK) <= q_chunk[None, :, None]
        k_sel = jax.vmap(lambda kb, ib: kb[ib])(k, sel)
        v_sel = jax.vmap(lambda vb, ib: vb[ib])(v, sel)
        logits = jnp.einsum('bqhd,bqkhd->bqhk', qb.astype(jnp.float32),
                            k_sel.astype(jnp.float32)) * scale
        logits = jnp.where(valid[:, :, None, :], logits, -jnp.inf)
        p = jax.nn.softmax(logits, axis=-1)
        return jnp.einsum('bqhk,bqkhd->bqhd', p, v_sel.astype(jnp.float32)).astype(q.dtype)

    out = lax.map(block_fn, (blocks(q), blocks(iq), blocks(iw32), q_pos))
    return jnp.moveaxis(out, 0, 1).reshape(B, S, H, Dh)


def hybrid_layer(x, c, norm_w, w_ada, b_ada, w_in, mu, w0, w_up, a0, a_up, k_k, k_a,
                 r_k, gn_w, gn_b, q_gain, k_gain, w_a_out, w_b_out, w_o):
    B, S, D = x.shape
    mod = jax.nn.silu(c) @ w_ada + b_ada
    shift, scale, gate = jnp.split(mod, 3, axis=-1)
    h = rms_norm(x, norm_w) * (1.0 + scale[:, None, :]) + shift[:, None, :]
    p = h @ w_in
    pa = p[..., :N_RWKV_COLS]
    pb = p[..., N_RWKV_COLS:N_RWKV_COLS + N_DSA_COLS]
    pg = p[..., N_RWKV_COLS + N_DSA_COLS:]
    ya, ga = rwkv7_time_mix(pa, mu, w0, w_up, a0, a_up, k_k, k_a, r_k, gn_w, gn_b)
    ya = (ya * jax.nn.silu(ga)) @ w_a_out
    q, k, v, gb, iq, ik, iw = jnp.split(pb, DSA_SPLITS, axis=-1)
    heads = lambda t: t.reshape(B, S, ATT_HEADS, HEAD_DIM)
    q = rms_norm(heads(q), q_gain)
    k = rms_norm(heads(k), k_gain)
    v = heads(v)
    iq = iq.reshape(B, S, IDX_HEADS, IDX_DIM)
    yb = dsa_sparse_attention(q, k, v, iq, ik, iw).reshape(B, S, D_ATT)
    yb = (yb * jax.nn.silu(gb)) @ w_b_out
    gm_a, gm_b = jnp.split(pg, 2, axis=-1)
    merged = jax.nn.sigmoid(gm_a) * ya + jax.nn.sigmoid(gm_b) * yb
    out = merged @ w_o
    return x + gate[:, None, :] * out


def setup_inputs(seed: int = 0) -> dict:
    key = jax.random.key(seed)
    ks = jax.random.split(key, 24)
    nrm = lambda k, shape, s: jax.random.normal(k, shape, jnp.float32) * s
    L = DEPTH
    return {
        "x": nrm(ks[0], (BATCH, SEQ, D_MODEL), 1.0),
        "c": nrm(ks[1], (BATCH, D_MODEL), 1.0),
        "norm_w": 1.0 + nrm(ks[2], (L, D_MODEL), 0.02),
        "w_ada": nrm(ks[3], (L, D_MODEL, 3 * D_MODEL), 0.5 * D_MODEL ** -0.5),
        "b_ada": nrm(ks[4], (L, 3 * D_MODEL), 0.02),
        "w_in": nrm(ks[5], (L, D_MODEL, N_IN), D_MODEL ** -0.5),
        "mu": jax.random.uniform(ks[6], (L, N_RWKV_COLS), jnp.float32),
        "w0": nrm(ks[7], (L, D_RWKV), 0.5),
        "w_up": nrm(ks[8], (L, DECAY_LORA, D_RWKV), 0.5 * DECAY_LORA ** -0.5),
        "a0": nrm(ks[9], (L, D_RWKV), 0.1),
        "a_up": nrm(ks[10], (L, ICLR_LORA, D_RWKV), 0.5 * ICLR_LORA ** -0.5),
        "k_k": 0.85 + nrm(ks[11], (L, D_RWKV), 0.05),
        "k_a": 1.0 + nrm(ks[12], (L, D_RWKV), 0.05),
        "r_k": nrm(ks[13], (L, RWKV_HEADS, HEAD_DIM), 0.1),
        "gn_w": 1.0 + nrm(ks[14], (L, D_RWKV), 0.02),
        "gn_b": nrm(ks[15], (L, D_RWKV), 0.02),
        "q_gain": 1.0 + nrm(ks[16], (L, HEAD_DIM), 0.02),
        "k_gain": 1.0 + nrm(ks[17], (L, HEAD_DIM), 0.02),
        "w_a_out": nrm(ks[18], (L, D_RWKV, D_MODEL), D_RWKV ** -0.5),
        "w_b_out": nrm(ks[19], (L, D_ATT, D_MODEL), D_ATT ** -0.5),
        "w_o": nrm(ks[20], (L, D_MODEL, D_MODEL), D_MODEL ** -0.5),
    }


def reference(x, c, norm_w, w_ada, b_ada, w_in, mu, w0, w_up, a0, a_up, k_k, k_a,
              r_k, gn_w, gn_b, q_gain, k_gain, w_a_out, w_b_out, w_o):
    for l in range(DEPTH):
        x = hybrid_layer(x, c, norm_w[l], w_ada[l], b_ada[l], w_in[l], mu[l], w0[l],
                         w_up[l], a0[l], a_up[l], k_k[l], k_k[l] * 0.0 + k_a[l] if False else k_a[l],
                         r_k[l], gn_w[l], gn_b[l], q_gain[l], k_gain[l],
                         w_a_out[l], w_b_out[l], w_o[l])
    return x
```

```python
import os
from contextlib import ExitStack

import numpy as np
import concourse.bass as bass
import concourse.mybir as mybir
from concourse.bass_utils import run_bass_kernel_spmd

F32 = mybir.dt.float32
BF16 = mybir.dt.bfloat16
I32 = mybir.dt.int32
AF = mybir.ActivationFunctionType
ALU = mybir.AluOpType
AX = mybir.AxisListType

NCORES = 8
S = 16384
D = 1024
TPC = 2048
NBL = 16
N_IN = 6856
NA_COLS = 2176
NB0 = 2176
NG0 = 4808
DECAY_SCALE = 0.606531
GN_EPS = 64e-5
NORM_EPS = 1e-6
NEG = -1.0e30


class Track:
    def __init__(self, sem, name, is_dma):
        self.sem = sem
        self.name = name
        self.cnt = 0
        self.is_dma = is_dma
        self.epoch = 0
        self.waiters = []


class TT:
    def __init__(self, h, name=""):
        self.h = h
        self.name = name
        self.lw = None
        self.rd = []

    def __getitem__(self, idx):
        return self.h[idx]


SEM_LIMIT_ENG = 3000
LAZY_INC = True
SEM_LIMIT_DMA = 1000


class Prog:
    ENGS = ("pe", "act", "dve", "pool", "sp")

    def __init__(self, nc, stack):
        self.nc = nc
        self.stack = stack
        self.q = {e: [] for e in self.ENGS}
        self.tr = {}
        self.nsem = 0
        for e in self.ENGS:
            self.tr[e] = Track(self.new_sem("s_" + e), e, False)
        self.dma_tracks = []
        self.known = {e: {} for e in self.ENGS}
        self.n_inst = 0
        self.waited = set()

    def new_sem(self, name):
        self.nsem += 1
        return self.stack.enter_context(self.nc.semaphore("%s_%d" % (name, self.nsem)))

    def dma_track(self, name):
        t = Track(self.new_sem("sd_" + name), name, True)
        self.dma_tracks.append(t)
        return t

    def sb(self, name, shape, dt=F32, stack=None):
        st = stack if stack is not None else self.stack
        h = st.enter_context(self.nc.sbuf_tensor(name, list(shape), dt))
        return TT(h, name)

    def ps(self, name, shape, dt=F32, stack=None):
        st = stack if stack is not None else self.stack
        h = st.enter_context(self.nc.psum_tensor(name, list(shape), dt))
        return TT(h, name)

    def dram(self, name, shape, dt=F32, kind=None):
        if kind is None:
            h = self.nc.dram_tensor(name, list(shape), dt)
        else:
            h = self.nc.dram_tensor(name, list(shape), dt, kind=kind)
        return TT(h, name)

    def emit(self, eng, fn, reads=(), writes=(), track=None, inc=None):
        waits = {}

        def add(dep):
            t, v, ep = dep
            if ep != t.epoch:
                return
            if t.is_dma:
                v = t.cnt
            if t not in waits or waits[t] < v:
                waits[t] = v

        for t in reads:
            if t.lw is not None:
                add(t.lw)
        for t in writes:
            if t.lw is not None:
                add(t.lw)
            for r in t.rd:
                add(r)
        mytrack = track if track is not None else self.tr[eng]
        if mytrack.is_dma:
            for dep in mytrack.waiters:
                add(dep)
            mytrack.waiters = []
        wl = []
        for t, v in waits.items():
            if eng == "pe" and t is self.tr["pe"]:
                continue
            if self.known[eng].get(t, 0) >= v:
                continue
            self.known[eng][t] = v
            wl.append((t.sem, v))
            self.waited.add((id(t.sem), v))
        if inc is None:
            inc = 16 if mytrack.is_dma else 1
        mytrack.cnt += inc
        val = mytrack.cnt
        self.q[eng].append((wl, fn, mytrack.sem, inc, val, mytrack.is_dma))
        self.n_inst += 1
        dep = (mytrack, val, mytrack.epoch)
        for t in waits:
            if t.is_dma and t is not mytrack:
                t.waiters.append(dep)
                if len(t.waiters) > 32:
                    best = {}
                    for tr_, v_, ep_ in t.waiters:
                        if ep_ == tr_.epoch and (tr_ not in best or best[tr_][1] < v_):
                            best[tr_] = (tr_, v_, ep_)
                    t.waiters = list(best.values())
        for t in reads:
            t.rd.append(dep)
            if len(t.rd) > 48:
                best = {}
                for tr_, v_, ep_ in t.rd:
                    if ep_ != tr_.epoch:
                        continue
                    if tr_ not in best or best[tr_][1] < v_:
                        best[tr_] = (tr_, v_, ep_)
                t.rd = list(best.values())
        for t in writes:
            t.lw = dep
            t.rd = []
        return dep

    def barrier(self):
        alltr = [self.tr[e] for e in self.ENGS] + self.dma_tracks
        for e in self.ENGS:
            wl = []
            for t in alltr:
                if t.cnt == 0 or t is self.tr[e]:
                    continue
                if self.known[e].get(t, 0) >= t.cnt:
                    continue
                self.known[e][t] = t.cnt
                wl.append((t.sem, t.cnt))
                self.waited.add((id(t.sem), t.cnt))
            if wl:
                self.q[e].append((wl, None, None, 0, 0, False))
        for t in alltr:
            lim = SEM_LIMIT_DMA if t.is_dma else SEM_LIMIT_ENG
            if t.cnt > lim:
                t.sem = self.new_sem("se_" + t.name)
                t.cnt = 0
                t.epoch += 1
                for e in self.ENGS:
                    self.known[e].pop(t, None)

    def build(self):
        nc = self.nc
        with nc.Block() as block:
            def mk(ename):
                def body(engine):
                    pending = {}
                    for wl, fn, sem, inc, val, is_dma in self.q[ename]:
                        for s_, v in wl:
                            engine.wait_ge(s_, v)
                        if fn is None:
                            continue
                        if is_dma:
                            fn(engine).then_inc(sem, inc)
                            continue
                        pend = pending.get(id(sem), 0) + inc
                        if (not LAZY_INC) or (id(sem), val) in self.waited or pend >= 16:
                            fn(engine).then_inc(sem, pend)
                            pend = 0
                        else:
                            fn(engine)
                        pending[id(sem)] = pend
                return body

            block.tensor(mk("pe"))
            block.scalar(mk("act"))
            block.vector(mk("dve"))
            block.gpsimd(mk("pool"))
            block.sync(mk("sp"))


class K:
    pass


def build_program(debug=()):
    nc = bass.Bass("TRN2", target_bir_lowering=False)
    st = ExitStack()
    with st:
        P = Prog(nc, st)
        k = K()
        k.P = P
        k.nc = nc
        k.debug = set(debug)
        k.dbg_out = {}
        declare_io(k)
        alloc_global(k)
        phase1(k)
        if "stop1" not in k.debug:
            gather1(k)
            phase2a(k)
            if "stop2a" not in k.debug:
                phase2b(k)
                if "stop2b" not in k.debug:
                    gather2(k)
                    phase3(k)
        finish(k)
        P.build()
    return nc, k


def dbg_tensor(k, name, shape, dt=F32):
    t = k.P.dram("dbg_" + name, shape, dt, kind="ExternalOutput")
    k.dbg_out[name] = t
    return t


def declare_io(k):
    P = k.P
    inp = lambda n, s, dt=F32: P.dram(n, s, dt, kind="ExternalInput")
    k.x = inp("x", [TPC, D])
    k.cT = inp("cT", [128, 8])
    k.norm_wT = inp("norm_wT", [128, 8])
    k.b_adaT = inp("b_adaT", [128, 24])
    k.w_ada = inp("w_ada", [D, 3 * D])
    k.w_in = inp("w_in", [D, N_IN])
    k.ident = inp("ident", [128, 128])
    k.gain2 = inp("gain2", [128, 2])
    k.bones = inp("bones", [128, 128])
    k.out = P.dram("out", [TPC, D], F32, kind="ExternalOutput")
    k.stage_a = P.dram("stage_a", [TPC, NA_COLS], F32)
    k.ag_a = P.dram("ag_a", [NCORES * TPC, NA_COLS], F32)
    k.stage_k = P.dram("stage_k", [TPC, 512], BF16)
    k.ag_k = P.dram("ag_k", [NCORES * TPC, 512], BF16)
    k.stage_v = P.dram("stage_v", [TPC, 512], BF16)
    k.ag_v = P.dram("ag_v", [NCORES * TPC, 512], BF16)
    k.stage_ik = P.dram("stage_ik", [64, TPC], BF16)
    k.ag_ik = P.dram("ag_ik", [NCORES * 64, TPC], BF16)
    k.gbs = P.dram("gbs", [TPC, 512], F32)
    k.stage_ya = P.dram("stage_ya", [64, S], F32)
    k.gate_d = P.dram("gate_d", [1, D], F32)
    k.w_a_out = inp("w_a_out", [512, D])
    k.w_b_out = inp("w_b_out", [512, D])
    k.w_o = inp("w_o", [D, D])
    k.yb_d = P.dram("yb_d", [TPC, 512], F32)
    k.dmask = inp("dmask", [128, 1024])
    k.rwl = P.dram("rwl", [S, 256], F32)
    k.hT_d = P.dram("hT_d", [128, 8, TPC], BF16)
    k.qT_d = P.dram("qT_d", [128, 4, TPC], BF16)
    k.iqT_d = P.dram("iqT_d", [128, 4, TPC], BF16)
    k.ag_ya = P.dram("ag_ya", [NCORES * 64, S], F32)
    k.cid = inp("cid", [1, 4], I32)
    k.mu4_d = inp("mu4", [128, 4, 384])
    k.rwp_d = inp("rwp", [64, 8])
    k.wup_d = inp("wup", [64, 64])
    k.aup_d = inp("aup", [64, 64])
    k.M4_d = inp("M4", [128, 4, 128])
    k.NAm_d = inp("NAm", [64, 4, 128])
    k.rst_d = inp("rst", [64, 512])
    k.id8_d = inp("id8", [64, 8, 64])


def alloc_global(k):
    P = k.P
    k.ld = [P.dma_track("ld%d" % i) for i in range(4)]
    k.stt = P.dma_track("st")
    k.cc = P.dma_track("cc")
    k.cst = P.dma_track("cst")
    k.ident_s = P.sb("ident_s", [128, 128])
    k.identb_s = P.sb("identb_s", [128, 128], BF16)
    k.iw = P.sb("iw", [128, NBL, 8])
    k.modT = P.sb("modT", [128, 24])
    k.gain2_s = P.sb("gain2_s", [128, 2])
    k.bones_s = P.sb("bones_s", [128, 128])
    k.bank = [P.ps("bank%d" % i, [128, 512]) for i in range(8)]
    P.emit("sp", lambda e: e.dma_start(out=k.ident_s[:], in_=k.ident[:]), reads=[k.ident], writes=[k.ident_s], track=k.cst)
    P.emit("sp", lambda e: e.dma_start(out=k.gain2_s[:], in_=k.gain2[:]), reads=[k.gain2], writes=[k.gain2_s], track=k.cst)
    P.emit("sp", lambda e: e.dma_start(out=k.bones_s[:], in_=k.bones[:]), reads=[k.bones], writes=[k.bones_s], track=k.cst)
    P.emit("dve", lambda e: e.tensor_copy(out=k.identb_s[:], in_=k.ident_s[:]), reads=[k.ident_s], writes=[k.identb_s])


def w_in_view(k):
    return k.w_in.h.ap().rearrange("(kc p) c -> p kc c", p=128)


def load_weights_bf16(k, ph, name, c0, ncols, wst, eng_cycle):
    P = k.P
    wb = P.sb(name, [128, 8, ncols], BF16, stack=ph)
    wv = w_in_view(k)
    off = 0
    i = 0
    while off < ncols:
        n = min(512, ncols - off)
        stg = wst[i % 2]
        trk = k.ld[i % 2]
        P.emit("sp", lambda e, stg=stg, off=off, n=n: e.dma_start(out=stg[:, :, 0:n], in_=wv[:, :, c0 + off:c0 + off + n]),
               reads=[k.w_in], writes=[stg], track=trk)
        eng = eng_cycle[i % len(eng_cycle)]
        if eng == "act":
            P.emit("act", lambda e, stg=stg, off=off, n=n: e.copy(out=wb[:, :, off:off + n], in_=stg[:, :, 0:n]), reads=[stg], writes=[wb])
        else:
            P.emit(eng, lambda e, stg=stg, off=off, n=n: e.tensor_copy(out=wb[:, :, off:off + n], in_=stg[:, :, 0:n]), reads=[stg], writes=[wb])
        off += n
        i += 1
    return wb


def phase1(k):
    P = k.P
    nc = k.nc
    p1s = ExitStack()
    k.hT = P.sb("hT", [128, 8, TPC], BF16, stack=p1s)
    k.qT = P.sb("qT", [128, 4, TPC], BF16, stack=p1s)
    k.iqT = P.sb("iqT", [128, 4, TPC], BF16, stack=p1s)
    with ExitStack() as ph:
        wada = P.sb("wada", [128, 8, 3 * D], F32, stack=ph)
        cT_s = P.sb("cT_s", [128, 8], stack=ph)
        sc = P.sb("sc", [128, 8], stack=ph)
        bT = P.sb("bT", [128, 24], stack=ph)
        wav = k.w_ada.h.ap().rearrange("(kc p) c -> p kc c", p=128)
        P.emit("sp", lambda e: e.dma_start(out=cT_s[:], in_=k.cT[:]), reads=[k.cT], writes=[cT_s], track=k.cst)
        P.emit("sp", lambda e: e.dma_start(out=bT[:], in_=k.b_adaT[:]), reads=[k.b_adaT], writes=[bT], track=k.cst)
        for kc in range(8):
            P.emit("sp" if kc % 2 == 0 else "pool", lambda e, kc=kc: e.dma_start(out=wada[:, kc, :], in_=wav[:, kc, :]),
                   reads=[k.w_ada], writes=[wada], track=k.ld[kc % 2])
        P.emit("act", lambda e: e.activation(out=sc[:], in_=cT_s[:], func=AF.Silu), reads=[cT_s], writes=[sc])
        mod_ps = k.bank[0]
        for m in range(24):
            for kc in range(8):
                P.emit("pe", lambda e, m=m, kc=kc: e.matmul(mod_ps[:, m:m + 1], lhsT=wada[:, kc, m * 128:(m + 1) * 128], rhs=sc[:, kc:kc + 1],
                                                            start=(kc == 0), stop=(kc == 7)), reads=[wada, sc], writes=[mod_ps])
        P.emit("dve", lambda e: e.tensor_tensor(out=k.modT[:], in0=mod_ps[:, 0:24], in1=bT[:], op=ALU.add), reads=[mod_ps, bT], writes=[k.modT])
        P.barrier()
    with ExitStack() as ph:
        nw = P.sb("nw", [128, 8], stack=ph)
        Asc = P.sb("Asc", [128, 8], stack=ph)
        P.emit("sp", lambda e: e.dma_start(out=nw[:], in_=k.norm_wT[:]), reads=[k.norm_wT], writes=[nw], track=k.cst)
        P.emit("dve", lambda e: e.scalar_tensor_tensor(out=Asc[:], in0=k.modT[:, 8:16], scalar=1.0, in1=nw[:], op0=ALU.add, op1=ALU.mult),
               reads=[k.modT, nw], writes=[Asc])
        xt = [P.sb("xt%d" % i, [128, 4, D], stack=ph) for i in range(2)]
        xn = [P.sb("xn%d" % i, [128, 4, D], stack=ph) for i in range(2)]
        junk = P.sb("junk1", [128, D], stack=ph)
        ss = [P.sb("ss%d" % i, [128, 4], stack=ph) for i in range(2)]
        rstd = [P.sb("rstd%d" % i, [128, 4], stack=ph) for i in range(2)]
        xv = k.x.h.ap().rearrange("(g t p) f -> g p t f", p=128, t=4)
        tps = [k.bank[1], k.bank[2]]
        for g in range(4):
            b = g % 2
            for t in range(4):
                P.emit("sp" if t % 2 == 0 else "pool", lambda e, g=g, t=t, b=b: e.dma_start(out=xt[b][:, t, :], in_=xv[g][:, t, :]),
                       reads=[k.x], writes=[xt[b]], track=k.ld[2 + b])
            for t in range(4):
                P.emit("act", lambda e, t=t, b=b: e.activation(out=junk[:], in_=xt[b][:, t, :], func=AF.Square, accum_out=ss[b][:, t:t + 1]),
                       reads=[xt[b]], writes=[junk, ss[b]])
            P.emit("dve", lambda e, b=b: e.tensor_scalar(out=rstd[b][:], in0=ss[b][:], scalar1=1.0 / D, scalar2=NORM_EPS, op0=ALU.mult, op1=ALU.add),
                   reads=[ss[b]], writes=[rstd[b]])
            P.emit("act", lambda e, b=b: e.activation(out=rstd[b][:], in_=rstd[b][:], func=AF.Sqrt), reads=[rstd[b]], writes=[rstd[b]])
            P.emit("dve", lambda e, b=b: e.reciprocal(out=rstd[b][:], in_=rstd[b][:]), reads=[rstd[b]], writes=[rstd[b]])
            for t in range(4):
                P.emit("dve" if t % 2 == 0 else "pool", lambda e, t=t, b=b: e.tensor_scalar(out=xn[b][:, t, :], in0=xt[b][:, t, :], scalar1=rstd[b][:, t:t + 1], scalar2=None, op0=ALU.mult),
                       reads=[xt[b], rstd[b]], writes=[xn[b]])
            for kc in range(8):
                tp = tps[kc % 2]
                for t in range(4):
                    P.emit("pe", lambda e, t=t, kc=kc, b=b, tp=tp: e.transpose(out=tp[:, t * 128:(t + 1) * 128], in_=xn[b][:, t, kc * 128:(kc + 1) * 128], identity=k.ident_s[:]),
                           reads=[xn[b], k.ident_s], writes=[tp])
                P.emit("act", lambda e, kc=kc, g=g, tp=tp: e.activation(out=k.hT[:, kc, g * 512:(g + 1) * 512], in_=tp[:], func=AF.Identity,
                                                                         scale=Asc[:, kc:kc + 1], bias=k.modT[:, kc:kc + 1]),
                       reads=[tp, Asc, k.modT], writes=[k.hT])
        if "hT" in k.debug:
            d = dbg_tensor(k, "hT", [128, 8, TPC], BF16)
            P.emit("sp", lambda e, d=d: e.dma_start(out=d[:], in_=k.hT[:]), reads=[k.hT], writes=[d], track=k.stt)
        P.barrier()
    with ExitStack() as ph:
        wst = [P.sb("wstA%d" % i, [128, 8, 512], stack=ph) for i in range(2)]
        wb = load_weights_bf16(k, ph, "wbA", 0, NA_COLS, wst, ["dve", "pool"])
        stg = [P.sb("stgA%d" % i, [128, NA_COLS], stack=ph) for i in range(2)]
        pbank = [k.bank[i] for i in range(3, 8)]
        nev = 0
        for t in range(NBL):
            sg = stg[t % 2]
            sgv = sg[:, 0:2048].rearrange("p (h q d) -> p h q d", h=8, q=4)
            for blk in range(5):
                src0 = [0, 512, 1024, 1664, 1536][blk]
                n = 512 if blk < 4 else 128
                pb = pbank[(t * 5 + blk) % 5]
                for kc in range(8):
                    P.emit("pe", lambda e, t=t, kc=kc, pb=pb, src0=src0, n=n: e.matmul(pb[:, 0:n], lhsT=k.hT[:, kc, t * 128:(t + 1) * 128], rhs=wb[:, kc, src0:src0 + n],
                                                                                        start=(kc == 0), stop=(kc == 7)), reads=[k.hT, wb], writes=[pb])
                eng = "act" if nev % 2 == 0 else "dve"
                nev += 1
                if blk < 4:
                    oap = sgv[:, :, blk, :]
                    iap = pb[:, 0:512].rearrange("p (h d) -> p h d", h=8)
                else:
                    oap = sg[:, 2048:2176]
                    iap = pb[:, 0:128]
                if eng == "act":
                    P.emit("act", lambda e, oap=oap, iap=iap: e.copy(out=oap, in_=iap), reads=[pb], writes=[sg])
                else:
                    P.emit("dve", lambda e, oap=oap, iap=iap: e.tensor_copy(out=oap, in_=iap), reads=[pb], writes=[sg])
            P.emit("pool", lambda e, t=t, sg=sg: e.dma_start(out=k.stage_a[t * 128:(t + 1) * 128, :], in_=sg[:]), reads=[sg], writes=[k.stage_a], track=k.stt)
        P.barrier()
        P.emit("pool", lambda e: e.collective_compute("AllGather", ALU.bypass, replica_groups=[list(range(NCORES))], ins=[k.stage_a[:]], outs=[k.ag_a[:]]),
               reads=[k.stage_a], writes=[k.ag_a], track=k.cc, inc=1)
    with ExitStack() as ph:
        wst = [P.sb("wstB%d" % i, [128, 8, 512], stack=ph) for i in range(2)]
        wqk = load_weights_bf16(k, ph, "wqk", NB0, 1024, wst, ["dve", "pool"])
        sq = [P.sb("sq%d" % i, [128, 512], stack=ph) for i in range(2)]
        rs = [P.sb("rs%d" % i, [128, 512], stack=ph) for i in range(2)]
        kst = [P.sb("kst%d" % i, [128, 512], BF16, stack=ph) for i in range(2)]
        it = 0
        for m8 in range(8):
            for g in range(4):
                b = it % 2
                pq = k.bank[it % 2]
                pss = k.bank[2 + it % 2]
                for kc in range(8):
                    P.emit("pe", lambda e, m8=m8, g=g, kc=kc, pq=pq: e.matmul(pq[:], lhsT=wqk[:, kc, m8 * 128:(m8 + 1) * 128], rhs=k.hT[:, kc, g * 512:(g + 1) * 512],
                                                                              start=(kc == 0), stop=(kc == 7)), reads=[wqk, k.hT], writes=[pq])
                P.emit("act", lambda e, b=b, pq=pq: e.activation(out=sq[b][:], in_=pq[:], func=AF.Square), reads=[pq], writes=[sq[b]])
                P.emit("pe", lambda e, b=b, pss=pss: e.matmul(pss[:], lhsT=k.bones_s[:], rhs=sq[b][:], start=True, stop=True), reads=[k.bones_s, sq[b]], writes=[pss])
                P.emit("act", lambda e, b=b, pss=pss: e.activation(out=rs[b][:], in_=pss[:], func=AF.Sqrt, scale=1.0 / 64, bias=NORM_EPS), reads=[pss], writes=[rs[b]])
                P.emit("dve", lambda e, b=b: e.reciprocal(out=rs[b][:], in_=rs[b][:]), reads=[rs[b]], writes=[rs[b]])
                gi = 0 if m8 < 4 else 1
                if m8 < 4:
                    P.emit("dve", lambda e, b=b, pq=pq, m8=m8, g=g: e.scalar_tensor_tensor(out=k.qT[:, m8, g * 512:(g + 1) * 512], in0=pq[:], scalar=k.gain2_s[:, 0:1], in1=rs[b][:], op0=ALU.mult, op1=ALU.mult),
                           reads=[pq, k.gain2_s, rs[b]], writes=[k.qT])
                else:
                    P.emit("dve", lambda e, b=b, pq=pq: e.scalar_tensor_tensor(out=kst[b][:], in0=pq[:], scalar=k.gain2_s[:, 1:2], in1=rs[b][:], op0=ALU.mult, op1=ALU.mult),
                           reads=[pq, k.gain2_s, rs[b]], writes=[kst[b]])
                    skv = k.stage_k.h.ap().rearrange("(j p) (m t) -> p j m t", p=128, m=4)
                    P.emit("pool", lambda e, b=b, m8=m8, g=g, skv=skv: e.dma_start(out=skv[:, 4 * g:4 * g + 4, m8 - 4, :], in_=kst[b][:].rearrange("p (j t) -> p j t", j=4)),
                           reads=[kst[b]], writes=[k.stage_k], track=k.stt)
                it += 1
        P.barrier()
    with ExitStack() as ph:
        wst = [P.sb("wstC%d" % i, [128, 8, 512], stack=ph) for i in range(2)]
        wvg = load_weights_bf16(k, ph, "wvg", NB0 + 1024, 1024, wst, ["dve", "pool"])
        vst = [P.sb("vst%d" % i, [128, 512], BF16, stack=ph) for i in range(2)]
        gst = [P.sb("gst%d" % i, [128, 512], stack=ph) for i in range(2)]
        for t in range(NBL):
            b = t % 2
            pv = k.bank[(2 * t) % 4]
            pg = k.bank[(2 * t + 1) % 4]
            for kc in range(8):
                P.emit("pe", lambda e, t=t, kc=kc, pv=pv: e.matmul(pv[:], lhsT=k.hT[:, kc, t * 128:(t + 1) * 128], rhs=wvg[:, kc, 0:512], start=(kc == 0), stop=(kc == 7)),
                       reads=[k.hT, wvg], writes=[pv])
            for kc in range(8):
                P.emit("pe", lambda e, t=t, kc=kc, pg=pg: e.matmul(pg[:], lhsT=k.hT[:, kc, t * 128:(t + 1) * 128], rhs=wvg[:, kc, 512:1024], start=(kc == 0), stop=(kc == 7)),
                       reads=[k.hT, wvg], writes=[pg])
            P.emit("dve", lambda e, b=b, pv=pv: e.tensor_copy(out=vst[b][:], in_=pv[:]), reads=[pv], writes=[vst[b]])
            P.emit("act", lambda e, b=b, pg=pg: e.activation(out=gst[b][:], in_=pg[:], func=AF.Silu), reads=[pg], writes=[gst[b]])
            P.emit("pool", lambda e, b=b, t=t: e.dma_start(out=k.stage_v[t * 128:(t + 1) * 128, :], in_=vst[b][:]), reads=[vst[b]], writes=[k.stage_v], track=k.stt)
            P.emit("pool", lambda e, b=b, t=t: e.dma_start(out=k.gbs[t * 128:(t + 1) * 128, :], in_=gst[b][:]), reads=[gst[b]], writes=[k.gbs], track=k.stt)
        P.barrier()
    with ExitStack() as ph:
        wst = [P.sb("wstD%d" % i, [128, 8, 512], stack=ph) for i in range(2)]
        wiq = load_weights_bf16(k, ph, "wiq", NB0 + 2048, 584, wst, ["dve", "pool"])
        ikst = [P.sb("ikst%d" % i, [64, 512], BF16, stack=ph) for i in range(2)]
        it = 0
        for m in range(4):
            for g in range(4):
                pq = k.bank[it % 4]
                for kc in range(8):
                    P.emit("pe", lambda e, m=m, g=g, kc=kc, pq=pq: e.matmul(pq[:], lhsT=wiq[:, kc, m * 128:(m + 1) * 128], rhs=k.hT[:, kc, g * 512:(g + 1) * 512],
                                                                            start=(kc == 0), stop=(kc == 7)), reads=[wiq, k.hT], writes=[pq])
                if it % 2 == 0:
                    P.emit("act", lambda e, m=m, g=g, pq=pq: e.copy(out=k.iqT[:, m, g * 512:(g + 1) * 512], in_=pq[:]), reads=[pq], writes=[k.iqT])
                else:
                    P.emit("dve", lambda e, m=m, g=g, pq=pq: e.tensor_copy(out=k.iqT[:, m, g * 512:(g + 1) * 512], in_=pq[:]), reads=[pq], writes=[k.iqT])
                it += 1
        for g in range(4):
            b = g % 2
            pq = k.bank[4 + g % 2]
            for kc in range(8):
                P.emit("pe", lambda e, g=g, kc=kc, pq=pq: e.matmul(pq[0:64, :], lhsT=wiq[:, kc, 512:576], rhs=k.hT[:, kc, g * 512:(g + 1) * 512],
                                                                   start=(kc == 0), stop=(kc == 7)), reads=[wiq, k.hT], writes=[pq])
            P.emit("act", lambda e, b=b, pq=pq: e.copy(out=ikst[b][:], in_=pq[0:64, :]), reads=[pq], writes=[ikst[b]])
            P.emit("pool", lambda e, b=b, g=g: e.dma_start(out=k.stage_ik[:, g * 512:(g + 1) * 512], in_=ikst[b][:]), reads=[ikst[b]], writes=[k.stage_ik], track=k.stt)
        pw = k.bank[6]
        for t in range(NBL):
            for kc in range(8):
                P.emit("pe", lambda e, t=t, kc=kc: e.matmul(pw[:, t * 8:(t + 1) * 8], lhsT=k.hT[:, kc, t * 128:(t + 1) * 128], rhs=wiq[:, kc, 576:584],
                                                            start=(kc == 0), stop=(kc == 7)), reads=[k.hT, wiq], writes=[pw])
        P.emit("dve", lambda e: e.tensor_copy(out=k.iw[:].rearrange("p t h -> p (t h)"), in_=pw[:, 0:128]), reads=[pw], writes=[k.iw])
        P.barrier()
    for src, dst in [(k.hT, k.hT_d), (k.qT, k.qT_d), (k.iqT, k.iqT_d)]:
        P.emit("sp", lambda e, src=src, dst=dst: e.dma_start(out=dst[:], in_=src[:]), reads=[src], writes=[dst], track=k.stt)
    if "p1" in k.debug:
        P = k.P
        for nm, src, shp, dt in [("stage_a", k.stage_a, [TPC, NA_COLS], F32), ("stage_k", k.stage_k, [TPC, 512], BF16),
                                 ("stage_v", k.stage_v, [TPC, 512], BF16), ("stage_ik", k.stage_ik, [64, TPC], BF16), ("gbs", k.gbs, [TPC, 512], F32)]:
            d = dbg_tensor(k, nm, shp, dt)
            P.emit("pool", lambda e, d=d, src=src: e.dma_start(out=d[:], in_=src[:]), reads=[src], writes=[d], track=k.stt)
        for nm, src, shp, dt in [("qT", k.qT, [128, 4, TPC], BF16), ("iqT", k.iqT, [128, 4, TPC], BF16), ("iw", k.iw, [128, NBL, 8], F32), ("modT", k.modT, [128, 24], F32)]:
            d = dbg_tensor(k, nm, shp, dt)
            P.emit("sp", lambda e, d=d, src=src: e.dma_start(out=d[:], in_=src[:]), reads=[src], writes=[d], track=k.stt)
    P.barrier()
    p1s.close()


def gather1(k):
    P = k.P
    for src, dst in [(k.stage_k, k.ag_k), (k.stage_v, k.ag_v), (k.stage_ik, k.ag_ik)]:
        P.emit("pool", lambda e, src=src, dst=dst: e.collective_compute("AllGather", ALU.bypass, replica_groups=[list(range(NCORES))],
                                                                         ins=[src[:]], outs=[dst[:]]),
               reads=[src], writes=[dst], track=k.cc, inc=1)


def phase2a(k):
    P = k.P
    nc = k.nc
    with ExitStack() as ph:
        sb = lambda n, s, dt=F32: P.sb(n, s, dt, stack=ph)
        mu4 = sb("mu4_s", [128, 4, 384])
        rwp = sb("rwp_s", [64, 8])
        wup = sb("wup_s", [64, 64]); aup = sb("aup_s", [64, 64])
        M4 = sb("M4_s", [128, 4, 128]); NAm = sb("NAm_s", [64, 4, 128]); rst = sb("rst_s", [64, 512]); id8 = sb("id8_s", [64, 8, 64])
        omka = sb("omka", [64, 1]); neghalf = sb("neghalf", [64, 512])
        for dst, src in [(mu4, k.mu4_d), (rwp, k.rwp_d), (wup, k.wup_d), (aup, k.aup_d), (M4, k.M4_d), (NAm, k.NAm_d), (rst, k.rst_d), (id8, k.id8_d)]:
            P.emit("sp", lambda e, dst=dst, src=src: e.dma_start(out=dst[:], in_=src[:]), reads=[src], writes=[dst], track=k.cst)
        P.emit("dve", lambda e: e.tensor_scalar(out=omka[:], in0=rwp[:, 3:4], scalar1=-1.0, scalar2=1.0, op0=ALU.mult, op1=ALU.add), reads=[rwp], writes=[omka])
        P.emit("pool", lambda e: e.memset(neghalf[:], -0.5), writes=[neghalf])
        W0, A0, KK, KA, RK, GW, GB = [rwp[:, i:i + 1] for i in range(7)]
        ones64 = k.bones_s[0:64, 0:64]
        id64 = k.ident_s[0:64, 0:64]
        reg = ph.enter_context(nc.sync.register("hreg"))
        stt_ = {}

        def ldreg(e):
            ins = e.reg_load(reg, k.cid[0:1, 0:1])
            return ins

        def ldreg2(e):
            ins = e.reg_load(reg, k.cid[0:1, 0:1])
            stt_["hoff"] = e.snap(reg, min_val=0, max_val=7 * 256)
            return ins
        P.emit("sp", ldreg, reads=[k.cid])
        P.emit("sp", ldreg2, reads=[k.cid])
        for q in range(8):
            P.emit("sp", lambda e, q=q: e.dma_start(out=k.rwl[q * 2048:(q + 1) * 2048, :], in_=k.ag_a[q * 2048:(q + 1) * 2048, bass.ds(stt_["hoff"], 256)]),
                   reads=[k.ag_a], writes=[k.rwl], track=k.stt)
        agv = k.ag_a.h.ap().rearrange("(r j i) c -> i r j c", r=8, j=16)
        rwv = k.rwl.h.ap().rearrange("(r j i) c -> i r j c", r=8, j=16)
        raw = [sb("raw%d" % i, [128, 4, 384]) for i in range(2)]
        prv = [sb("prv%d" % i, [128, 4, 384]) for i in range(2)]
        xs = sb("xs", [128, 4, 384])
        names = ["rT", "kT", "vT", "adT", "twd", "sgm", "wT", "invw", "kkraw", "kk2", "ssc", "rn", "kkh", "t1", "k2", "rk", "gam", "invg", "gprev", "bt",
                 "ysb", "y2", "mm_", "dlt", "msq", "var", "rstd", "yn", "o1"]
        T = {n: sb(n, [64, 512]) for n in names}
        aT = sb("aT", [64, 512])
        sgT = [sb("sgT%d" % i, [64, 512]) for i in range(2)]
        bonus = [sb("bonus%d" % i, [64, 512]) for i in range(2)]
        ar = [sb("ar%d" % i, [64, 8, 128]) for i in range(2)]
        kb = sb("kb", [64, 8, 128]); kbp = sb("kbp", [64, 8, 128])
        AAm = [sb("AAm%d" % i, [128, 8, 128]) for i in range(2)]
        PQ = [sb("PQ%d" % i, [64, 8, 128]) for i in range(2)]
        TTs = [sb("TTs%d" % i, [64, 8, 128]) for i in range(2)]
        V2 = [sb("V2_%d" % i, [128, 8, 64]) for i in range(2)]
        K2p = [sb("K2p%d" % i, [128, 8, 64]) for i in range(2)]
        Dg = [sb("Dg%d" % i, [64, 8, 64]) for i in range(2)]
        H = [sb("H%d" % i, [64, 64]) for i in range(2)]
        Zs = sb("Zs", [64, 64])
        outT = [sb("outT%d" % i, [64, 512]) for i in range(2)]
        for i in range(2):
            P.emit("pool", lambda e, i=i: e.memset(TTs[i][:], 0.0), writes=[TTs[i]])
        P.emit("pool", lambda e: e.memset(H[0][:], 0.0), writes=[H[0]])
        P.emit("pool", lambda e: e.memset(prv[0][0:1, 0, :], 0.0), writes=[prv[0]])
        tq = [k.bank[0], k.bank[1]]
        tqi = [0]

        def nexttq():
            tqi[0] += 1
            return tq[tqi[0] % 2]
        bAA = [k.bank[2], k.bank[3]]
        Yp = [k.bank[4], k.bank[5]]
        TUp = k.bank[6]
        Zp = TT(k.bank[7][:, 0:64], "Zp"); Up = TT(k.bank[7][:, 64:128], "Up"); Hp = TT(k.bank[7][:, 128:192], "Hp")
        v3 = lambda t: t[:].rearrange("p (c t) -> p c t", c=8)
        yav = k.stage_ya.h.ap().rearrange("ch (r j i) -> ch r j i", r=8, j=16)
        NG = int(os.environ.get("RWKV_NG", "32")) if "short2a" not in k.debug else 2
        hstate = [0]

        def g_loads(gi):
            par = gi % 2
            b0 = 4 * gi
            j = b0 // 8
            r0 = b0 % 8
            R, PV = raw[par], prv[par]
            trk = k.ld[par]
            A_ = ar[par]; D_ = Dg[par]; AA_ = AAm[par]; TT_ = TTs[par]; V_ = V2[par]; K_ = K2p[par]; Y_ = Yp[par]; O_ = outT[par]
            P.emit("sp", lambda e, R=R, j=j, r0=r0: e.dma_start(out=R[:, :, 0:256], in_=rwv[:, r0:r0 + 4, j, :]), reads=[k.rwl], writes=[R], track=trk)
            P.emit("sp", lambda e, R=R, j=j, r0=r0: e.dma_start(out=R[:, :, 256:384], in_=agv[:, r0:r0 + 4, j, 2048:2176]), reads=[k.ag_a], writes=[R], track=trk)
            P.emit("sp", lambda e, PV=PV, j=j, r0=r0: e.dma_start(out=PV[1:128, :, 0:256], in_=rwv[0:127, r0:r0 + 4, j, :]), reads=[k.rwl], writes=[PV], track=trk)
            P.emit("sp", lambda e, PV=PV, j=j, r0=r0: e.dma_start(out=PV[1:128, :, 256:384], in_=agv[0:127, r0:r0 + 4, j, 2048:2176]), reads=[k.ag_a], writes=[PV], track=trk)
            if r0 == 4:
                segs = [(0, 4, 3, j)]
            elif j > 0:
                segs = [(0, 1, 7, j - 1), (1, 4, 0, j)]
            else:
                segs = [(1, 4, 0, j)]
            for (t0, t1, rs_, jj) in segs:
                n = t1 - t0
                P.emit("sp", lambda e, PV=PV, t0=t0, t1=t1, rs_=rs_, jj=jj, n=n: e.dma_start(out=PV[0:1, t0:t1, 0:256], in_=rwv[127:128, rs_:rs_ + n, jj, :]),
                       reads=[k.rwl], writes=[PV], track=trk)
                P.emit("sp", lambda e, PV=PV, t0=t0, t1=t1, rs_=rs_, jj=jj, n=n: e.dma_start(out=PV[0:1, t0:t1, 256:384], in_=agv[127:128, rs_:rs_ + n, jj, 2048:2176]),
                       reads=[k.ag_a], writes=[PV], track=trk)
            if False:
                yield

        def g_prep(gi):
            par = gi % 2
            b0 = 4 * gi
            j = b0 // 8
            r0 = b0 % 8
            R, PV = raw[par], prv[par]
            trk = k.ld[par]
            A_ = ar[par]; D_ = Dg[par]; AA_ = AAm[par]; TT_ = TTs[par]; V_ = V2[par]; K_ = K2p[par]; Y_ = Yp[par]; O_ = outT[par]
            yield
            P.emit("dve", lambda e, R=R, PV=PV: e.tensor_tensor(out=PV[:], in0=PV[:], in1=R[:], op=ALU.subtract), reads=[PV, R], writes=[PV])
            yield
            P.emit("dve", lambda e, PV=PV: e.tensor_tensor(out=PV[:], in0=PV[:], in1=mu4[:], op=ALU.mult), reads=[PV, mu4], writes=[PV])
            yield
            P.emit("dve", lambda e, R=R, PV=PV: e.tensor_tensor(out=xs[:], in0=PV[:], in1=R[:], op=ALU.add), reads=[PV, R], writes=[xs])
            dsts = [("rT", None), ("kT", None), ("vT", None), ("sg", AF.Silu), ("twd", AF.Tanh), ("adT", None)]
            yield
            for qi, (nm, fn) in enumerate(dsts):
                pb = nexttq()
                yield
                for tt in range(4):
                    yield
                    P.emit("pe", lambda e, pb=pb, tt=tt, qi=qi: e.transpose(out=pb[0:64, tt * 128:(tt + 1) * 128], in_=xs[:, tt, qi * 64:(qi + 1) * 64], identity=k.ident_s[:]),
                           reads=[xs, k.ident_s], writes=[pb])
                dst = sgT[par] if nm == "sg" else T[nm]
                if fn is None:
                    yield
                    P.emit("act", lambda e, pb=pb, dst=dst: e.copy(out=dst[:], in_=pb[0:64, :]), reads=[pb], writes=[dst])
                else:
                    yield
                    P.emit("act", lambda e, pb=pb, dst=dst, fn=fn: e.activation(out=dst[:], in_=pb[0:64, :], func=fn), reads=[pb], writes=[dst])
            yield
            pb = nexttq()
            yield
            P.emit("pe", lambda e, pb=pb: e.matmul(pb[0:64, :], lhsT=wup[:], rhs=T["twd"][:], start=True, stop=True), reads=[wup, T["twd"]], writes=[pb])
            yield
            P.emit("act", lambda e, pb=pb: e.activation(out=T["sgm"][:], in_=pb[0:64, :], func=AF.Sigmoid, bias=W0), reads=[pb, rwp], writes=[T["sgm"]])
            yield
            P.emit("act", lambda e: e.activation(out=T["wT"][:], in_=T["sgm"][:], func=AF.Exp, scale=-DECAY_SCALE), reads=[T["sgm"]], writes=[T["wT"]])
            yield
            P.emit("act", lambda e: e.activation(out=T["invw"][:], in_=T["sgm"][:], func=AF.Exp, scale=DECAY_SCALE), reads=[T["sgm"]], writes=[T["invw"]])
            yield
            pb = nexttq()
            yield
            P.emit("pe", lambda e, pb=pb: e.matmul(pb[0:64, :], lhsT=aup[:], rhs=T["adT"][:], start=True, stop=True), reads=[aup, T["adT"]], writes=[pb])
            yield
            P.emit("act", lambda e, pb=pb: e.activation(out=aT[:], in_=pb[0:64, :], func=AF.Sigmoid, bias=A0), reads=[pb, rwp], writes=[aT])
            yield
            P.emit("dve", lambda e: e.tensor_scalar(out=T["kkraw"][:], in0=T["kT"][:], scalar1=KK, scalar2=None, op0=ALU.mult), reads=[T["kT"], rwp], writes=[T["kkraw"]])
            yield
            P.emit("dve", lambda e: e.tensor_tensor(out=T["kk2"][:], in0=T["kkraw"][:], in1=T["kkraw"][:], op=ALU.mult), reads=[T["kkraw"]], writes=[T["kk2"]])
            yield
            pb = nexttq()
            yield
            P.emit("pe", lambda e, pb=pb: e.matmul(pb[0:64, :], lhsT=ones64, rhs=T["kk2"][:], start=True, stop=True), reads=[k.bones_s, T["kk2"]], writes=[pb])
            yield
            P.emit("dve", lambda e, pb=pb: e.tensor_scalar(out=T["ssc"][:], in0=pb[0:64, :], scalar1=1e-19, scalar2=None, op0=ALU.max), reads=[pb], writes=[T["ssc"]])
            yield
            P.emit("act", lambda e: e.activation(out=T["rn"][:], in_=T["ssc"][:], func=AF.Ln), reads=[T["ssc"]], writes=[T["rn"]])
            yield
            P.emit("act", lambda e: e.activation(out=T["rn"][:], in_=T["rn"][:], func=AF.Exp, scale=-0.5), reads=[T["rn"]], writes=[T["rn"]])
            yield
            P.emit("dve", lambda e: e.tensor_tensor(out=T["kkh"][:], in0=T["kkraw"][:], in1=T["rn"][:], op=ALU.mult), reads=[T["kkraw"], T["rn"]], writes=[T["kkh"]])
            yield
            P.emit("dve", lambda e: e.tensor_scalar(out=T["t1"][:], in0=aT[:], scalar1=KA, scalar2=omka[:, 0:1], op0=ALU.mult, op1=ALU.add), reads=[aT, rwp, omka], writes=[T["t1"]])
            yield
            P.emit("dve", lambda e: e.tensor_tensor(out=T["k2"][:], in0=T["kT"][:], in1=T["t1"][:], op=ALU.mult), reads=[T["kT"], T["t1"]], writes=[T["k2"]])
            yield
            P.emit("dve", lambda e: e.scalar_tensor_tensor(out=T["rk"][:], in0=T["rT"][:], scalar=RK, in1=T["k2"][:], op0=ALU.mult, op1=ALU.mult), reads=[T["rT"], rwp, T["k2"]], writes=[T["rk"]])
            yield
            pb = nexttq()
            yield
            P.emit("pe", lambda e, pb=pb: e.matmul(pb[0:64, :], lhsT=ones64, rhs=T["rk"][:], start=True, stop=True), reads=[k.bones_s, T["rk"]], writes=[pb])
            yield
            P.emit("dve", lambda e, pb=pb, par=par: e.tensor_tensor(out=bonus[par][:], in0=pb[0:64, :], in1=T["vT"][:], op=ALU.mult), reads=[pb, T["vT"]], writes=[bonus[par]])
            yield
            P.emit("dve", lambda e: e.tensor_tensor_scan(out=T["gam"][:], data0=rst[:], data1=T["wT"][:], initial=0.0, op0=ALU.max, op1=ALU.mult), reads=[rst, T["wT"]], writes=[T["gam"]])
            yield
            P.emit("dve", lambda e: e.reciprocal(out=T["invg"][:], in_=T["gam"][:]), reads=[T["gam"]], writes=[T["invg"]])
            yield
            P.emit("dve", lambda e: e.tensor_tensor(out=T["gprev"][:], in0=T["gam"][:], in1=T["invw"][:], op=ALU.mult), reads=[T["gam"], T["invw"]], writes=[T["gprev"]])
            A_ = ar[par]
            yield
            P.emit("dve", lambda e, A_=A_: e.scalar_tensor_tensor(out=A_[:, :, 0:64], in0=v3(T["kkh"]), scalar=-1.0, in1=v3(T["gprev"]), op0=ALU.mult, op1=ALU.mult), reads=[T["kkh"], T["gprev"]], writes=[A_])
            yield
            P.emit("dve", lambda e, A_=A_: e.tensor_tensor(out=A_[:, :, 64:128], in0=v3(T["rT"]), in1=v3(T["gam"]), op=ALU.mult), reads=[T["rT"], T["gam"]], writes=[A_])
            yield
            P.emit("dve", lambda e: e.tensor_tensor(out=kb[:, :, 0:64], in0=v3(T["k2"]), in1=v3(T["invg"]), op=ALU.mult), reads=[T["k2"], T["invg"]], writes=[kb])
            yield
            P.emit("dve", lambda e: e.tensor_tensor(out=T["bt"][:], in0=T["kkh"][:], in1=aT[:], op=ALU.mult), reads=[T["kkh"], aT], writes=[T["bt"]])
            yield
            P.emit("dve", lambda e: e.tensor_tensor(out=kb[:, :, 64:128], in0=v3(T["bt"]), in1=v3(T["invg"]), op=ALU.mult), reads=[T["bt"], T["invg"]], writes=[kb])
            gC = v3(T["gam"])[:, :, 63:64]
            yield
            P.emit("dve", lambda e, gC=gC: e.tensor_tensor(out=kbp[:], in0=kb[:], in1=gC.to_broadcast([64, 8, 128]), op=ALU.mult), reads=[kb, T["gam"]], writes=[kbp])
            D_ = Dg[par]
            yield
            P.emit("dve", lambda e, gC=gC, D_=D_: e.tensor_tensor(out=D_[:], in0=id8[:], in1=gC.to_broadcast([64, 8, 64]), op=ALU.mult), reads=[id8, T["gam"]], writes=[D_])
            AA_ = AAm[par]
            yield
            for c in range(8):
                pb = bAA[c // 4]
                yield
                P.emit("pe", lambda e, pb=pb, c=c, A_=A_: e.matmul(pb[:, (c % 4) * 128:(c % 4 + 1) * 128], lhsT=kb[:, c, :], rhs=A_[:, c, :], start=True, stop=True), reads=[kb, A_], writes=[pb])
            yield
            for hb in range(2):
                yield
                P.emit("dve", lambda e, hb=hb, AA_=AA_: e.tensor_tensor(out=AA_[:, hb * 4:(hb + 1) * 4, :], in0=bAA[hb][:].rearrange("p (c t) -> p c t", c=4), in1=M4[:], op=ALU.mult),
                       reads=[bAA[hb], M4], writes=[AA_])
            yield
            for c in range(8):
                pb = bAA[c // 4]
                o = (c % 4) * 128
                yield
                P.emit("pe", lambda e, pb=pb, c=c, o=o, A_=A_: e.matmul(pb[0:64, o:o + 64], lhsT=A_[:, c, 0:64], rhs=kb[:, c, 64:128], start=True, stop=True), reads=[kb, A_], writes=[pb])
                yield
                P.emit("pe", lambda e, pb=pb, c=c, o=o, A_=A_: e.matmul(pb[0:64, o + 64:o + 128], lhsT=kb[:, c, 64:128], rhs=A_[:, c, 0:64], start=True, stop=True), reads=[kb, A_], writes=[pb])
            cur = PQ[0]
            yield
            for hb in range(2):
                yield
                P.emit("dve", lambda e, hb=hb, cur=cur: e.tensor_tensor(out=cur[:, hb * 4:(hb + 1) * 4, :], in0=bAA[hb][0:64, :].rearrange("p (c t) -> p c t", c=4), in1=NAm[:], op=ALU.mult),
                       reads=[bAA[hb], NAm], writes=[cur])
            TT_ = TTs[par]
            yield
            P.emit("dve", lambda e, cur=cur, TT_=TT_: e.tensor_tensor(out=TT_[:, :, 64:128], in0=cur[:, :, 64:128], in1=id8[:], op=ALU.add), reads=[cur, id8], writes=[TT_])
            yield
            for lv in range(1, 6):
                nxt = PQ[lv % 2]
                yield
                for c in range(8):
                    pb = bAA[c // 4]
                    o = (c % 4) * 128
                    yield
                    P.emit("pe", lambda e, pb=pb, c=c, o=o, cur=cur: e.matmul(pb[0:64, o:o + 64], lhsT=cur[:, c, 64:128], rhs=cur[:, c, 0:64], start=True, stop=True), reads=[cur], writes=[pb])
                    if lv < 5:
                        P.emit("pe", lambda e, pb=pb, c=c, o=o, cur=cur: e.matmul(pb[0:64, o + 64:o + 128], lhsT=cur[:, c, 0:64], rhs=cur[:, c, 64:128], start=True, stop=True), reads=[cur], writes=[pb])
                yield
                for hb in range(2):
                    if hb == 0:
                        P.emit("act", lambda e, hb=hb, nxt=nxt: e.copy(out=nxt[:, hb * 4:(hb + 1) * 4, :], in_=bAA[hb][0:64, :].rearrange("p (c t) -> p c t", c=4)), reads=[bAA[hb]], writes=[nxt])
                    else:
                        P.emit("dve", lambda e, hb=hb, nxt=nxt: e.tensor_copy(out=nxt[:, hb * 4:(hb + 1) * 4, :], in_=bAA[hb][0:64, :].rearrange("p (c t) -> p c t", c=4)), reads=[bAA[hb]], writes=[nxt])
                yield
                for c in range(8):
                    yield
                    P.emit("pe", lambda e, c=c, nxt=nxt, TT_=TT_: e.matmul(TUp[0:64, c * 64:(c + 1) * 64], lhsT=nxt[:, c, 0:64], rhs=TT_[:, c, 64:128], start=True, stop=True), reads=[nxt, TT_], writes=[TUp])
                yield
                P.emit("dve", lambda e, TT_=TT_: e.tensor_tensor(out=TT_[:, :, 64:128], in0=TT_[:, :, 64:128], in1=TUp[0:64, :].rearrange("p (c t) -> p c t", c=8), op=ALU.add), reads=[TT_, TUp], writes=[TT_])
                cur = nxt
            V_ = V2[par]
            yield
            pb = nexttq()
            yield
            for c in range(8):
                yield
                P.emit("pe", lambda e, pb=pb, c=c: e.transpose(out=pb[0:64, c * 64:(c + 1) * 64], in_=T["vT"][:, c * 64:(c + 1) * 64], identity=id64), reads=[T["vT"], k.ident_s], writes=[pb])
            yield
            P.emit("act", lambda e, pb=pb, V_=V_: e.copy(out=V_[0:64, :, :], in_=pb[0:64, :].rearrange("p (c t) -> p c t", c=8)), reads=[pb], writes=[V_])
            K_ = K2p[par]
            yield
            pb = nexttq()
            yield
            for c in range(8):
                yield
                P.emit("pe", lambda e, pb=pb, c=c: e.transpose(out=pb[:, c * 64:(c + 1) * 64], in_=kbp[:, c, :], identity=id64), reads=[kbp, k.ident_s], writes=[pb])
            yield
            P.emit("act", lambda e, pb=pb, K_=K_: e.copy(out=K_[:], in_=pb[:].rearrange("p (c t) -> p c t", c=8)), reads=[pb], writes=[K_])
            yield

        def g_chain(gi):
            par = gi % 2
            b0 = 4 * gi
            j = b0 // 8
            r0 = b0 % 8
            R, PV = raw[par], prv[par]
            trk = k.ld[par]
            A_ = ar[par]; D_ = Dg[par]; AA_ = AAm[par]; TT_ = TTs[par]; V_ = V2[par]; K_ = K2p[par]; Y_ = Yp[par]; O_ = outT[par]
            Y_ = Yp[par]
            yield
            for c in range(8):
                yield
                Hc, Hn = H[hstate[0]], H[1 - hstate[0]]
                yield
                P.emit("pe", lambda e, c=c, A_=A_, Hc=Hc: e.matmul(Zp[0:64, :], lhsT=A_[:, c, 0:64], rhs=Hc[:], start=True, stop=False), reads=[A_, Hc], writes=[Zp])
                yield
                P.emit("pe", lambda e, c=c, AA_=AA_, V_=V_: e.matmul(Zp[0:64, :], lhsT=AA_[0:64, c, 0:64], rhs=V_[0:64, c, :], start=False, stop=True), reads=[AA_, V_], writes=[Zp])
                yield
                P.emit("act", lambda e: e.copy(out=Zs[:], in_=Zp[0:64, :]), reads=[Zp], writes=[Zs])
                yield
                P.emit("pe", lambda e, c=c, TT_=TT_: e.matmul(Up[:], lhsT=TT_[:, c, :], rhs=Zs[:], start=True, stop=True), reads=[TT_, Zs], writes=[Up])
                yield
                P.emit("dve", lambda e, c=c, V_=V_: e.tensor_copy(out=V_[64:128, c, :], in_=Up[64:128, :]), reads=[Up], writes=[V_])
                yield
                P.emit("pe", lambda e, c=c, D_=D_, Hc=Hc: e.matmul(Hp[0:64, :], lhsT=D_[:, c, :], rhs=Hc[:], start=True, stop=False), reads=[D_, Hc], writes=[Hp])
                yield
                P.emit("pe", lambda e, c=c, K_=K_, V_=V_: e.matmul(Hp[0:64, :], lhsT=K_[:, c, :], rhs=V_[:, c, :], start=False, stop=True), reads=[K_, V_], writes=[Hp])
                yield
                P.emit("act", lambda e, Hn=Hn: e.copy(out=Hn[:], in_=Hp[0:64, :]), reads=[Hp], writes=[Hn])
                yield
                P.emit("pe", lambda e, c=c, A_=A_, Hc=Hc, Y_=Y_: e.matmul(Y_[0:64, c * 64:(c + 1) * 64], lhsT=Hc[:], rhs=A_[:, c, 64:128], start=True, stop=False), reads=[A_, Hc], writes=[Y_])
                yield
                P.emit("pe", lambda e, c=c, AA_=AA_, V_=V_, Y_=Y_: e.matmul(Y_[0:64, c * 64:(c + 1) * 64], lhsT=V_[:, c, :], rhs=AA_[:, c, 64:128], start=False, stop=True), reads=[AA_, V_], writes=[Y_])
                hstate[0] = 1 - hstate[0]
            yield
            P.emit("act", lambda e, Y_=Y_: e.copy(out=T["ysb"][:], in_=Y_[0:64, :]), reads=[Y_], writes=[T["ysb"]])
            yield
            P.emit("act", lambda e, Y_=Y_: e.activation(out=T["y2"][:], in_=Y_[0:64, :], func=AF.Square), reads=[Y_], writes=[T["y2"]])
            yield
            pb = nexttq()
            yield
            P.emit("pe", lambda e, pb=pb: e.matmul(pb[0:64, :], lhsT=ones64, rhs=T["ysb"][:], start=True, stop=True), reads=[k.bones_s, T["ysb"]], writes=[pb])
            yield
            P.emit("act", lambda e, pb=pb: e.activation(out=T["mm_"][:], in_=pb[0:64, :], func=AF.Copy, scale=1.0 / 64), reads=[pb], writes=[T["mm_"]])
            pb2 = nexttq()
            yield
            P.emit("pe", lambda e, pb2=pb2: e.matmul(pb2[0:64, :], lhsT=ones64, rhs=T["y2"][:], start=True, stop=True), reads=[k.bones_s, T["y2"]], writes=[pb2])
            yield
            P.emit("dve", lambda e: e.tensor_tensor(out=T["dlt"][:], in0=T["ysb"][:], in1=T["mm_"][:], op=ALU.subtract), reads=[T["ysb"], T["mm_"]], writes=[T["dlt"]])
            yield
            P.emit("dve", lambda e: e.tensor_tensor(out=T["msq"][:], in0=T["mm_"][:], in1=T["mm_"][:], op=ALU.mult), reads=[T["mm_"]], writes=[T["msq"]])
            yield
            P.emit("dve", lambda e, pb2=pb2: e.scalar_tensor_tensor(out=T["var"][:], in0=pb2[0:64, :], scalar=1.0 / 64, in1=T["msq"][:], op0=ALU.mult, op1=ALU.subtract), reads=[pb2, T["msq"]], writes=[T["var"]])
            yield
            P.emit("dve", lambda e: e.tensor_scalar(out=T["var"][:], in0=T["var"][:], scalar1=GN_EPS, scalar2=None, op0=ALU.add), reads=[T["var"]], writes=[T["var"]])
            yield
            P.emit("act", lambda e: e.activation(out=T["rstd"][:], in_=T["var"][:], func=AF.Ln), reads=[T["var"]], writes=[T["rstd"]])
            yield
            P.emit("act", lambda e: e.activation(out=T["rstd"][:], in_=T["rstd"][:], func=AF.Exp, scale=-0.5), reads=[T["rstd"]], writes=[T["rstd"]])
            yield
            P.emit("dve", lambda e: e.tensor_tensor(out=T["yn"][:], in0=T["dlt"][:], in1=T["rstd"][:], op=ALU.mult), reads=[T["dlt"], T["rstd"]], writes=[T["yn"]])
            yield
            P.emit("dve", lambda e: e.tensor_scalar(out=T["yn"][:], in0=T["yn"][:], scalar1=GW, scalar2=GB, op0=ALU.mult, op1=ALU.add), reads=[T["yn"], rwp], writes=[T["yn"]])
            yield
            P.emit("dve", lambda e, par=par: e.tensor_tensor(out=T["o1"][:], in0=T["yn"][:], in1=bonus[par][:], op=ALU.add), reads=[T["yn"], bonus[par]], writes=[T["o1"]])
            O_ = outT[par]
            yield
            P.emit("dve", lambda e, par=par, O_=O_: e.tensor_tensor(out=O_[:], in0=T["o1"][:], in1=sgT[par][:], op=ALU.mult), reads=[T["o1"], sgT[par]], writes=[O_])
            yield
            P.emit("pool", lambda e, O_=O_, j=j, r0=r0: e.dma_start(out=yav[:, r0:r0 + 4, j, :], in_=O_[:].rearrange("p (r i) -> p r i", r=4)), reads=[O_], writes=[k.stage_ya], track=k.stt)

            yield

        def drain(*gens):
            for g in gens:
                for _ in g:
                    pass

        def interleave(ga, gb, ratio=3):
            da = db = False
            while not (da and db):
                if not da:
                    try:
                        next(ga)
                    except StopIteration:
                        da = True
                for _ in range(ratio):
                    if db:
                        break
                    try:
                        next(gb)
                    except StopIteration:
                        db = True

        drain(g_loads(0))
        if NG > 1:
            drain(g_loads(1))
        drain(g_prep(0))
        for gi in range(NG):
            if gi > 0 and gi % 6 == 0:
                P.barrier()
            if gi + 2 < NG:
                drain(g_loads(gi + 2))
            if gi + 1 < NG:
                interleave(g_chain(gi), g_prep(gi + 1))
            else:
                drain(g_chain(gi))
        if "ya" in k.debug:
            d = dbg_tensor(k, "stage_ya", [64, S], F32)
            P.emit("pool", lambda e, d=d: e.dma_start(out=d[:], in_=k.stage_ya[:]), reads=[k.stage_ya], writes=[d], track=k.stt)
        P.barrier()


def phase2b(k):
    P = k.P
    NIT = 18
    with ExitStack() as ph:
        sb = lambda n, s_, dt=F32: P.sb(n, s_, dt, stack=ph)
        scores = sb("scores", [128, S])
        negm = sb("negm", [128, S], BF16)
        negmB = TT(negm.h, "negmB")
        dmask = sb("dmask_s", [128, 1024])
        tmin = sb("tmin", [128, 1024])
        kT = [sb("kTs%d" % i, [128, 8, 4, 128], BF16) for i in range(2)]
        v1 = [sb("v1s%d" % i, [128, 8, 8, 65], BF16) for i in range(2)]
        ik2 = [sb("ik2s%d" % i, [128, 8, 128], BF16) for i in range(2)]
        Rh = [sb("Rh%d" % i, [128, 512], BF16) for i in range(4)]
        PT = [sb("PT%d" % i, [128, 4, 128], BF16) for i in range(4)]
        dg = sb("dg", [128, 8, 128], BF16)
        iqJ = [sb("iqJ%d" % i, [128, 4, 128], BF16) for i in range(2)]
        qJ = [sb("qJ%d" % i, [128, 4, 128], BF16) for i in range(2)]
        gbJ = [sb("gbJ%d" % i, [128, 512]) for i in range(2)]
        osb = sb("osb", [128, 8, 65])
        rsum = sb("rsum", [128, 8])
        ybo = [sb("ybo%d" % i, [128, 8, 64]) for i in range(2)]
        st = {n: sb("bs_" + n, [128, 1]) for n in ["lo", "hi", "mid", "cnt", "ge", "d1", "d2", "mn1", "mn2"]}
        ldk = [P.dma_track("ldk%d" % i) for i in range(2)]
        ldv = [P.dma_track("ldv%d" % i) for i in range(2)]
        ldi = [P.dma_track("ldi%d" % i) for i in range(2)]
        ldq = [P.dma_track("ldq%d" % i) for i in range(2)]
        P.emit("sp", lambda e: e.dma_start(out=dmask[:], in_=k.dmask[:]), reads=[k.dmask], writes=[dmask], track=k.cst)
        zt = sb("zt", [128, 260], BF16)
        P.emit("pool", lambda e: e.memset(zt[:], 0.0), writes=[zt])
        for i in range(2):
            P.emit("pool", lambda e, i=i: e.memset(v1[i][:], 1.0), writes=[v1[i]])
        agk = k.ag_k.h.ap().rearrange("(r j p) c -> p r j c", r=8, j=16)
        agv = k.ag_v.h.ap().rearrange("(r j p) c -> p r j c", r=8, j=16)
        vraw = [sb("vraw%d" % i, [128, 8, 512], BF16) for i in range(2)]
        agi = k.ag_ik.h.ap().rearrange("(r d) (j t) -> d r j t", r=8, j=16)
        scb = [k.bank[i] for i in range(4)]
        wsb = [k.bank[4], k.bank[5]]
        oA, oB = k.bank[6], k.bank[7]
        nld = 0
        nsc = 0
        nws = 0
        nev = 0
        nst = 0
        npt = 0
        NJ = NBL if "short2b" not in k.debug else 2
        maskT_ps = k.bank[4][:].bitcast(BF16)
        maskT_pst = k.bank[4]
        mT = [sb("maskT%d" % i, [128, 8, 128], BF16) for i in range(2)]
        nsum = sb("bs_nsum", [128, 1])
        cnt = {"ik": 0, "kv": 0, "ev": 0}

        def load_ik(sj):
            ib = ik2[cnt["ik"] % 2]
            for half in range(2):
                P.emit("sp", lambda e, ib=ib, sj=sj, half=half: e.dma_start(out=ib[half * 64:(half + 1) * 64, :, :], in_=agi[:, :, sj, :]),
                       reads=[k.ag_ik], writes=[ib], track=ldi[cnt["ik"] % 2])
            cnt["ik"] += 1
            return ib

        def load_kv(sj):
            i = cnt["kv"] % 2
            kt, vt, vr = kT[i], v1[i], vraw[i]
            P.emit("sp", lambda e, kt=kt, sj=sj: e.dma_start(out=kt[:].rearrange("p r m t -> p r (m t)"), in_=agk[:, :, sj, :]), reads=[k.ag_k], writes=[kt], track=ldk[i])
            P.emit("sp", lambda e, vr=vr, sj=sj: e.dma_start(out=vr[:], in_=agv[:, :, sj, :]), reads=[k.ag_v], writes=[vr], track=ldv[i])
            P.emit("pool", lambda e, vr=vr, vt=vt: e.tensor_copy(out=vt[:, :, :, 0:64], in_=vr[:].rearrange("p r (h d) -> p r h d", h=8)), reads=[vr], writes=[vt])
            cnt["kv"] += 1
            return kt, vt

        for J in range(NJ):
            P.barrier()
            pj = J % 2
            P.emit("sp", lambda e, J=J, pj=pj: e.dma_start(out=iqJ[pj][:], in_=k.iqT_d[:, :, J * 128:(J + 1) * 128]), reads=[k.iqT_d], writes=[iqJ[pj]], track=ldq[pj])
            P.emit("sp", lambda e, J=J, pj=pj: e.dma_start(out=qJ[pj][:], in_=k.qT_d[:, :, J * 128:(J + 1) * 128]), reads=[k.qT_d], writes=[qJ[pj]], track=ldq[pj])
            P.emit("sp", lambda e, J=J, pj=pj: e.dma_start(out=gbJ[pj][:], in_=k.gbs[J * 128:(J + 1) * 128, :]), reads=[k.gbs], writes=[gbJ[pj]], track=ldq[pj])
            for h in range(8):
                P.emit("pool", lambda e, h=h, J=J: e.tensor_scalar(out=dg[:, h, :], in0=k.identb_s[:], scalar1=k.iw[:, J, h:h + 1], scalar2=None, op0=ALU.mult),
                       reads=[k.identb_s, k.iw], writes=[dg])
            ibs = {0: load_ik(0)}
            steps = [(sj, half, h) for sj in range(J + 1) for half in range(2) for h in range(8)]
            scs = {}

            def idx_qk(i):
                sj, half, h = steps[i]
                if half == 0 and h == 2 and sj + 1 <= J:
                    ibs[sj + 1] = load_ik(sj + 1)
                ib = ibs[sj]
                m, e_ = h // 2, h % 2
                sc = scb[i % 4]
                scs[i] = sc
                P.emit("pe", lambda e, sc=sc, m=m, e_=e_, ib=ib, half=half, pj=pj: e.matmul(sc[:], lhsT=iqJ[pj][e_ * 64:(e_ + 1) * 64, m, :],
                                                                                          rhs=ib[e_ * 64:(e_ + 1) * 64, half * 4:(half + 1) * 4, :], start=True, stop=True),
                       reads=[iqJ[pj], ib], writes=[sc])

            for i in range(min(2, len(steps))):
                idx_qk(i)
            ws = None
            for i, (sj, half, h) in enumerate(steps):
                if i + 2 < len(steps):
                    idx_qk(i + 2)
                sc = scs.pop(i)
                rh = Rh[i % 4]
                if h == 0:
                    ws = wsb[(i // 8) % 2]
                if i % 2 == 0:
                    P.emit("act", lambda e, sc=sc, rh=rh: e.activation(out=rh[:], in_=sc[:], func=AF.Relu), reads=[sc], writes=[rh])
                else:
                    P.emit("dve", lambda e, sc=sc, rh=rh: e.tensor_scalar(out=rh[:], in0=sc[:], scalar1=0.0, scalar2=None, op0=ALU.max), reads=[sc], writes=[rh])
                P.emit("pe", lambda e, ws=ws, h=h, rh=rh: e.matmul(ws[:], lhsT=dg[:, h, :], rhs=rh[:], start=(h == 0), stop=(h == 7)), reads=[dg, rh], writes=[ws])
                if h == 7:
                    c0 = sj * 1024 + half * 512
                    if sj == J:
                        P.emit("dve", lambda e, ws=ws, c0=c0, half=half: e.tensor_tensor(out=scores[:, c0:c0 + 512], in0=ws[:], in1=dmask[:, half * 512:(half + 1) * 512], op=ALU.add),
                               reads=[ws, dmask], writes=[scores])
                        P.emit("dve", lambda e, ws=ws, half=half: e.tensor_tensor(out=tmin[:, half * 512:(half + 1) * 512], in0=ws[:], in1=dmask[:, half * 512:(half + 1) * 512], op=ALU.subtract),
                               reads=[ws, dmask], writes=[tmin])
                    else:
                        if cnt["ev"] % 2 == 0:
                            P.emit("act", lambda e, ws=ws, c0=c0: e.copy(out=scores[:, c0:c0 + 512], in_=ws[:]), reads=[ws], writes=[scores])
                        else:
                            P.emit("dve", lambda e, ws=ws, c0=c0: e.tensor_copy(out=scores[:, c0:c0 + 512], in_=ws[:]), reads=[ws], writes=[scores])
                        cnt["ev"] += 1
            kvs = {0: load_kv(0)}
            N = (J + 1) * 1024
            N1 = (int(N * 0.45) // 128) * 128
            N2 = N - N1
            P.emit("dve", lambda e, N=N: e.tensor_reduce(out=st["hi"][:], in_=scores[:, 0:N], axis=AX.X, op=ALU.max), reads=[scores], writes=[st["hi"]])
            P.emit("dve", lambda e: e.tensor_reduce(out=st["lo"][:], in_=tmin[:], axis=AX.X, op=ALU.min), reads=[tmin], writes=[st["lo"]])
            if J > 0:
                P.emit("dve", lambda e, N=N: e.tensor_reduce(out=st["mn2"][:], in_=scores[:, 0:N - 1024], axis=AX.X, op=ALU.min), reads=[scores], writes=[st["mn2"]])
                P.emit("dve", lambda e: e.tensor_tensor(out=st["lo"][:], in0=st["lo"][:], in1=st["mn2"][:], op=ALU.min), reads=[st["lo"], st["mn2"]], writes=[st["lo"]])
            for it in range(NIT):
                P.emit("dve", lambda e: e.tensor_tensor(out=st["mid"][:], in0=st["lo"][:], in1=st["hi"][:], op=ALU.add), reads=[st["lo"], st["hi"]], writes=[st["mid"]])
                P.emit("dve", lambda e: e.tensor_scalar(out=st["mid"][:], in0=st["mid"][:], scalar1=0.5, scalar2=None, op0=ALU.mult), reads=[st["mid"]], writes=[st["mid"]])
                P.emit("dve", lambda e, N=N: e.tensor_scalar(out=negm[:, 0:N], in0=scores[:, 0:N], scalar1=st["mid"][:, 0:1], scalar2=0.0, op0=ALU.is_ge, op1=ALU.add, accum_out=st["cnt"][:, 0:1]),
                       reads=[scores, st["mid"]], writes=[negm, st["cnt"]])
                P.emit("dve", lambda e: e.tensor_scalar(out=st["ge"][:], in0=st["cnt"][:], scalar1=255.5, scalar2=None, op0=ALU.is_ge), reads=[st["cnt"]], writes=[st["ge"]])
                P.emit("dve", lambda e: e.tensor_tensor(out=st["d1"][:], in0=st["mid"][:], in1=st["lo"][:], op=ALU.subtract), reads=[st["mid"], st["lo"]], writes=[st["d1"]])
                P.emit("dve", lambda e: e.tensor_tensor(out=st["d2"][:], in0=st["hi"][:], in1=st["mid"][:], op=ALU.subtract), reads=[st["mid"], st["hi"]], writes=[st["d2"]])
                P.emit("dve", lambda e: e.scalar_tensor_tensor(out=st["lo"][:], in0=st["d1"][:], scalar=st["ge"][:, 0:1], in1=st["lo"][:], op0=ALU.mult, op1=ALU.add), reads=[st["d1"], st["ge"], st["lo"]], writes=[st["lo"]])
                P.emit("dve", lambda e: e.scalar_tensor_tensor(out=st["hi"][:], in0=st["d2"][:], scalar=st["ge"][:, 0:1], in1=st["mid"][:], op0=ALU.mult, op1=ALU.add), reads=[st["d2"], st["ge"], st["mid"]], writes=[st["hi"]])
            P.emit("dve", lambda e, N=N: e.tensor_scalar(out=negm[:, 0:N], in0=scores[:, 0:N], scalar1=st["lo"][:, 0:1], scalar2=-400.0, op0=ALU.is_lt, op1=ALU.mult),
                   reads=[scores, st["lo"], negmB], writes=[negm])
            if "thr" in k.debug:
                d = k.dbg_out.get("thr") or dbg_tensor(k, "thr", [128, NBL])
                P.emit("sp", lambda e, d=d, J=J: e.dma_start(out=d[:, J:J + 1], in_=st["lo"][:], allow_slow_non_contiguous=True), reads=[st["lo"]], writes=[d], track=k.stt)
            nkb = (J + 1) * 8
            for ob in (oA, oB):
                P.emit("pe", lambda e, ob=ob: e.matmul(ob[:, 0:260], lhsT=zt[:, 0:128], rhs=zt[:, 0:260], start=True, stop=False), reads=[zt], writes=[ob])
            asteps = [(sj, r, hb) for sj in range(J + 1) for r in range(8) for hb in range(2)]
            mts = {}
            stbs = {}

            def att_qk(i):
                sj, r, hb = asteps[i]
                if r == 0 and hb == 1 and sj + 1 <= J:
                    kvs[sj + 1] = load_kv(sj + 1)
                kt, vt = kvs[sj]
                stb = scb[i % 4]
                stbs[i] = stb
                for hh in range(4):
                    h = hb * 4 + hh
                    m, e_ = h // 2, h % 2
                    P.emit("pe", lambda e, stb=stb, hh=hh, kt=kt, r=r, m=m, e_=e_, pj=pj: e.matmul(stb[:, hh * 128:(hh + 1) * 128], lhsT=kt[e_ * 64:(e_ + 1) * 64, r, m, :],
                                                                                                 rhs=qJ[pj][e_ * 64:(e_ + 1) * 64, m, :], start=True, stop=False),
                           reads=[kt, qJ[pj]], writes=[stb])
                    c0 = sj * 1024 + r * 128
                    P.emit("pe", lambda e, stb=stb, hh=hh, c0=c0: e.matmul(stb[:, hh * 128:(hh + 1) * 128], lhsT=negm[:, c0:c0 + 128], rhs=k.identb_s[:], start=False, stop=True),
                           reads=[negm, k.identb_s], writes=[stb])

            att_qk(0)
            for i, (sj, r, hb) in enumerate(asteps):
                if i + 1 < len(asteps):
                    att_qk(i + 1)
                stb = stbs.pop(i)
                pt = PT[i % 4]
                kt, vt = kvs[sj]
                P.emit("act", lambda e, stb=stb, pt=pt: e.activation(out=pt[:], in_=stb[:].rearrange("p (h q) -> p h q", h=4), func=AF.Exp, scale=0.125), reads=[stb], writes=[pt])
                ob = oA if hb == 0 else oB
                kbi = sj * 8 + r
                for hh in range(4):
                    h = hb * 4 + hh
                    P.emit("pe", lambda e, ob=ob, hh=hh, h=h, pt=pt, vt=vt, r=r, kbi=kbi, nkb=nkb: e.matmul(ob[:, hh * 65:(hh + 1) * 65], lhsT=pt[:, hh, :], rhs=vt[:, r, h, :],
                                                                                                          start=False, stop=(kbi == nkb - 1 and hh == 3)),
                           reads=[pt, vt], writes=[ob])
            P.emit("act", lambda e: e.copy(out=osb[:, 0:4, :], in_=oA[:, 0:260].rearrange("p (h d) -> p h d", h=4)), reads=[oA], writes=[osb])
            P.emit("dve", lambda e: e.tensor_copy(out=osb[:, 4:8, :], in_=oB[:, 0:260].rearrange("p (h d) -> p h d", h=4)), reads=[oB], writes=[osb])
            P.emit("dve", lambda e: e.reciprocal(out=rsum[:], in_=osb[:, :, 64:65].rearrange("p h o -> p (h o)")), reads=[osb], writes=[rsum])
            yo = ybo[pj]
            P.emit("dve", lambda e, yo=yo: e.tensor_tensor(out=yo[:], in0=osb[:, :, 0:64], in1=rsum[:].unsqueeze(2).to_broadcast([128, 8, 64]), op=ALU.mult), reads=[osb, rsum], writes=[yo])
            P.emit("dve", lambda e, yo=yo, pj=pj: e.tensor_tensor(out=yo[:], in0=yo[:], in1=gbJ[pj][:].rearrange("p (h d) -> p h d", h=8), op=ALU.mult), reads=[yo, gbJ[pj]], writes=[yo])
            P.emit("pool", lambda e, yo=yo, J=J: e.dma_start(out=k.yb_d[J * 128:(J + 1) * 128, :], in_=yo[:].rearrange("p h d -> p (h d)")), reads=[yo], writes=[k.yb_d], track=k.stt)
        if "yb" in k.debug:
            d = dbg_tensor(k, "yb", [TPC, 512])
            P.emit("pool", lambda e, d=d: e.dma_start(out=d[:], in_=k.yb_d[:]), reads=[k.yb_d], writes=[d], track=k.stt)
        P.barrier()


def gather2(k):
    P = k.P
    P.emit("pool", lambda e: e.collective_compute("AllGather", ALU.bypass, replica_groups=[list(range(NCORES))], ins=[k.stage_ya[:]], outs=[k.ag_ya[:]]),
           reads=[k.stage_ya], writes=[k.ag_ya], track=k.cc, inc=1)


def load_w_bf16(k, wb, src_view, kchunks, ncols, wst, trks):
    P = k.P
    i = 0
    for off in range(0, ncols, 512):
        n = min(512, ncols - off)
        for k0 in range(0, kchunks, 8):
            kn = min(8, kchunks - k0)
            stg = wst[i % 2]
            P.emit("sp", lambda e, stg=stg, off=off, n=n, k0=k0, kn=kn: e.dma_start(out=stg[:, 0:kn, 0:n], in_=src_view[:, k0:k0 + kn, off:off + n]),
                   reads=[k.w_in], writes=[stg], track=trks[i % 2])
            eng = "dve" if i % 2 == 0 else "pool"
            P.emit(eng, lambda e, stg=stg, off=off, n=n, k0=k0, kn=kn: e.tensor_copy(out=wb[:, k0:k0 + kn, off:off + n], in_=stg[:, 0:kn, 0:n]), reads=[stg], writes=[wb])
            i += 1


def phase3(k):
    P = k.P
    nc = k.nc
    with ExitStack() as ph:
        sb = lambda n, s_, dt=F32: P.sb(n, s_, dt, stack=ph)
        Wa = sb("Wa", [128, 4, D], BF16); Wb = sb("Wb", [128, 4, D], BF16)
        Wg = sb("Wg", [128, 8, 2 * D], BF16); Wo = sb("Wo", [128, 8, D], BF16)
        hT = sb("hT3", [128, 8, TPC], BF16)
        yaT = sb("yaT", [128, 4, TPC], BF16); ybT = sb("ybT", [128, 4, TPC], BF16)
        gate_b = sb("gate_b", [128, D])
        P.emit("sp", lambda e: e.dma_start(out=hT[:], in_=k.hT_d[:]), reads=[k.hT_d], writes=[hT], track=k.cst)
        P.emit("pool", lambda e: e.dma_start(out=k.gate_d.h.ap().rearrange("o (j p) -> p (o j)", p=128), in_=k.modT[:, 16:24], allow_slow_non_contiguous=True),
               reads=[k.modT], writes=[k.gate_d], track=k.stt)
        P.emit("pool", lambda e: e.dma_start(out=gate_b[:], in_=k.gate_d[0:1, :].partition_broadcast(128)), reads=[k.gate_d], writes=[gate_b], track=k.stt)
        reg = ph.enter_context(nc.sync.register("creg"))
        stt_ = {}

        def ldreg(e):
            return e.reg_load(reg, k.cid[0:1, 1:2])

        def ldreg2(e):
            ins = e.reg_load(reg, k.cid[0:1, 1:2])
            stt_["coff"] = e.snap(reg, min_val=0, max_val=7 * TPC)
            return ins
        P.emit("sp", ldreg, reads=[k.cid])
        P.emit("sp", ldreg2, reads=[k.cid])
        with ExitStack() as ph2:
            wst = [P.sb("wst3_%d" % i, [128, 8, 512], stack=ph2) for i in range(2)]
            yaf = [P.sb("yaf%d" % i, [128, TPC], stack=ph2) for i in range(2)]
            ybt = [P.sb("ybt%d" % i, [128, 4, 512], stack=ph2) for i in range(2)]
            load_w_bf16(k, Wa, k.w_a_out.h.ap().rearrange("(kc p) c -> p kc c", p=128), 4, D, wst, k.ld[0:2])
            load_w_bf16(k, Wb, k.w_b_out.h.ap().rearrange("(kc p) c -> p kc c", p=128), 4, D, wst, k.ld[0:2])
            load_w_bf16(k, Wo, k.w_o.h.ap().rearrange("(kc p) c -> p kc c", p=128), 8, D, wst, k.ld[0:2])
            load_w_bf16(k, Wg, w_in_view(k)[:, :, NG0:NG0 + 2 * D], 8, 2 * D, wst, k.ld[0:2])
            for kb in range(4):
                yf = yaf[kb % 2]
                P.emit("sp", lambda e, yf=yf, kb=kb: e.dma_start(out=yf[:], in_=k.ag_ya[kb * 128:(kb + 1) * 128, bass.ds(stt_["coff"], TPC)]), reads=[k.ag_ya], writes=[yf], track=k.ld[2 + kb % 2])
                P.emit("act", lambda e, yf=yf, kb=kb: e.copy(out=yaT[:, kb, :], in_=yf[:]), reads=[yf], writes=[yaT])
            ybv = k.yb_d.h.ap().rearrange("(g t p) c -> g p t c", p=128, t=4)
            for G in range(4):
                yt = ybt[G % 2]
                P.emit("sp", lambda e, yt=yt, G=G: e.dma_start(out=yt[:], in_=ybv[G]), reads=[k.yb_d], writes=[yt], track=k.ld[2 + G % 2])
                for cb in range(4):
                    pb = k.bank[(G * 4 + cb) % 4]
                    for t in range(4):
                        P.emit("pe", lambda e, pb=pb, yt=yt, t=t, cb=cb: e.transpose(out=pb[:, t * 128:(t + 1) * 128], in_=yt[:, t, cb * 128:(cb + 1) * 128], identity=k.ident_s[:]),
                               reads=[yt, k.ident_s], writes=[pb])
                    if cb % 2 == 0:
                        P.emit("act", lambda e, pb=pb, G=G, cb=cb: e.copy(out=ybT[:, cb, G * 512:(G + 1) * 512], in_=pb[:]), reads=[pb], writes=[ybT])
                    else:
                        P.emit("dve", lambda e, pb=pb, G=G, cb=cb: e.tensor_copy(out=ybT[:, cb, G * 512:(G + 1) * 512], in_=pb[:]), reads=[pb], writes=[ybT])
            P.barrier()
        if "p3" in k.debug:
            d = dbg_tensor(k, "ag_ya", [NCORES * 64, S], F32)
            P.emit("sp", lambda e, d=d: e.dma_start(out=d[:], in_=k.ag_ya[:]), reads=[k.ag_ya], writes=[d], track=k.stt)
            d = dbg_tensor(k, "stage_ya2", [64, S], F32)
            P.emit("sp", lambda e, d=d: e.dma_start(out=d[:], in_=k.stage_ya[:]), reads=[k.stage_ya], writes=[d], track=k.stt)
            for nm, src, shp, dt in [("gate_b", gate_b, [128, D], F32), ("yaT", yaT, [128, 4, TPC], BF16), ("ybT", ybT, [128, 4, TPC], BF16), ("hT3", hT, [128, 8, TPC], BF16),
                                     ("Wa", Wa, [128, 4, D], BF16), ("Wo", Wo, [128, 8, D], BF16), ("Wg", Wg, [128, 8, 2 * D], BF16)]:
                d = dbg_tensor(k, nm, shp, dt)
                P.emit("sp", lambda e, d=d, src=src: e.dma_start(out=d[:], in_=src[:]), reads=[src], writes=[d], track=k.stt)
        mT = [sb("mT%d" % i, [128, 8, 512], BF16) for i in range(2)]
        sa = [sb("sa%d" % i, [128, 512]) for i in range(2)]
        sbb = [sb("sbb%d" % i, [128, 512]) for i in range(2)]
        m1 = [sb("m1_%d" % i, [128, 512]) for i in range(2)]
        m2 = [sb("m2_%d" % i, [128, 512]) for i in range(2)]
        xt = [sb("x3_%d" % i, [128, D]) for i in range(2)]
        rt = [sb("r3_%d" % i, [128, D]) for i in range(2)]
        it = 0
        nt = 0
        for G in range(4):
            M_ = mT[G % 2]
            tok = slice(G * 512, (G + 1) * 512)
            for cb in range(8):
                b = it % 2
                pYA, pYB, pGA, pGB = k.bank[0 + 4 * b], k.bank[1 + 4 * b], k.bank[2 + 4 * b], k.bank[3 + 4 * b]
                for kb in range(4):
                    P.emit("pe", lambda e, pYA=pYA, kb=kb, cb=cb, tok=tok: e.matmul(pYA[:], lhsT=Wa[:, kb, cb * 128:(cb + 1) * 128], rhs=yaT[:, kb, tok], start=(kb == 0), stop=(kb == 3)), reads=[Wa, yaT], writes=[pYA])
                for kb in range(4):
                    P.emit("pe", lambda e, pYB=pYB, kb=kb, cb=cb, tok=tok: e.matmul(pYB[:], lhsT=Wb[:, kb, cb * 128:(cb + 1) * 128], rhs=ybT[:, kb, tok], start=(kb == 0), stop=(kb == 3)), reads=[Wb, ybT], writes=[pYB])
                for kc in range(8):
                    P.emit("pe", lambda e, pGA=pGA, kc=kc, cb=cb, tok=tok: e.matmul(pGA[:], lhsT=Wg[:, kc, cb * 128:(cb + 1) * 128], rhs=hT[:, kc, tok], start=(kc == 0), stop=(kc == 7)), reads=[Wg, hT], writes=[pGA])
                for kc in range(8):
                    P.emit("pe", lambda e, pGB=pGB, kc=kc, cb=cb, tok=tok: e.matmul(pGB[:], lhsT=Wg[:, kc, D + cb * 128:D + (cb + 1) * 128], rhs=hT[:, kc, tok], start=(kc == 0), stop=(kc == 7)), reads=[Wg, hT], writes=[pGB])
                P.emit("act", lambda e, b=b, pGA=pGA: e.activation(out=sa[b][:], in_=pGA[:], func=AF.Sigmoid), reads=[pGA], writes=[sa[b]])
                P.emit("act", lambda e, b=b, pGB=pGB: e.activation(out=sbb[b][:], in_=pGB[:], func=AF.Sigmoid), reads=[pGB], writes=[sbb[b]])
                P.emit("dve", lambda e, b=b, pYA=pYA: e.tensor_tensor(out=m1[b][:], in0=pYA[:], in1=sa[b][:], op=ALU.mult), reads=[pYA, sa[b]], writes=[m1[b]])
                P.emit("dve", lambda e, b=b, pYB=pYB: e.tensor_tensor(out=m2[b][:], in0=pYB[:], in1=sbb[b][:], op=ALU.mult), reads=[pYB, sbb[b]], writes=[m2[b]])
                P.emit("pool", lambda e, b=b, M_=M_, cb=cb: e.tensor_tensor(out=M_[:, cb, :], in0=m1[b][:], in1=m2[b][:], op=ALU.add), reads=[m1[b], m2[b]], writes=[M_])
                it += 1
            if "p3" in k.debug and G == 0:
                d = dbg_tensor(k, "mT0", [128, 8, 512], BF16)
                P.emit("sp", lambda e, d=d, M_=M_: e.dma_start(out=d[:], in_=M_[:]), reads=[M_], writes=[d], track=k.stt)
            for t in range(4):
                lt = G * 4 + t
                xb_ = xt[nt % 2]
                rb_ = rt[nt % 2]
                P.emit("sp", lambda e, xb_=xb_, lt=lt: e.dma_start(out=xb_[:], in_=k.x[lt * 128:(lt + 1) * 128, :]), reads=[k.x], writes=[xb_], track=k.ld[2 + nt % 2])
                for nb in range(2):
                    po = k.bank[(2 * nt + nb) % 8]
                    for cb in range(8):
                        P.emit("pe", lambda e, po=po, cb=cb, nb=nb, t=t, M_=M_: e.matmul(po[:], lhsT=M_[:, cb, t * 128:(t + 1) * 128], rhs=Wo[:, cb, nb * 512:(nb + 1) * 512], start=(cb == 0), stop=(cb == 7)),
                               reads=[M_, Wo], writes=[po])
                    P.emit("dve", lambda e, po=po, nb=nb, rb_=rb_: e.tensor_tensor(out=rb_[:, nb * 512:(nb + 1) * 512], in0=po[:], in1=gate_b[:, nb * 512:(nb + 1) * 512], op=ALU.mult), reads=[po, gate_b], writes=[rb_])
                P.emit("pool", lambda e, rb_=rb_, xb_=xb_: e.tensor_tensor(out=rb_[:], in0=rb_[:], in1=xb_[:], op=ALU.add), reads=[rb_, xb_], writes=[rb_])
                P.emit("sp", lambda e, rb_=rb_, lt=lt: e.dma_start(out=k.out[lt * 128:(lt + 1) * 128, :], in_=rb_[:]), reads=[rb_], writes=[k.out], track=k.stt)
                nt += 1
        P.barrier()


def finish(k):
    P = k.P
    outs = [k.out] + list(k.dbg_out.values())
    P.barrier()


def make_in_maps(inputs):
    f = lambda a: np.ascontiguousarray(np.asarray(a), dtype=np.float32)
    x = f(inputs["x"])[0]
    xb = x.reshape(NBL, NCORES, 128, D)
    c = f(inputs["c"])[0]
    common = {
        "cT": np.ascontiguousarray(c.reshape(8, 128).T),
        "norm_wT": np.ascontiguousarray(f(inputs["norm_w"])[0].reshape(8, 128).T),
        "b_adaT": np.ascontiguousarray(f(inputs["b_ada"])[0].reshape(24, 128).T),
        "w_ada": f(inputs["w_ada"])[0],
        "w_in": f(inputs["w_in"])[0],
        "w_a_out": f(inputs["w_a_out"])[0],
        "w_b_out": f(inputs["w_b_out"])[0],
        "w_o": f(inputs["w_o"])[0],
        "ident": np.eye(128, dtype=np.float32),
        "gain2": np.ascontiguousarray(np.stack([np.tile(f(inputs["q_gain"])[0], 2), np.tile(f(inputs["k_gain"])[0], 2)], axis=1)),
        "bones": np.kron(np.eye(2, dtype=np.float32), np.ones((64, 64), np.float32)),
    }
    tt = np.arange(64)
    lower_strict = (tt[:, None] < tt[None, :]).astype(np.float32)
    lower_incl = (tt[:, None] <= tt[None, :]).astype(np.float32)
    M4 = np.block([[lower_strict, lower_incl], [lower_strict, lower_incl]]).astype(np.float32)
    NAm = np.concatenate([lower_strict.T, lower_strict], axis=1)
    rst = np.zeros((64, 512), np.float32); rst[:, ::64] = 1.0
    common["M4"] = np.ascontiguousarray(np.broadcast_to(M4[:, None, :], (128, 4, 128)))
    common["NAm"] = np.ascontiguousarray(np.broadcast_to(NAm[:, None, :], (64, 4, 128)))
    common["rst"] = rst
    common["id8"] = np.ascontiguousarray(np.broadcast_to(np.eye(64, dtype=np.float32)[:, None, :], (64, 8, 64)))
    mu = f(inputs["mu"])[0]
    hs = lambda v, h: v[h * 64:(h + 1) * 64]
    maps = []
    for cid in range(NCORES):
        m = dict(common)
        m["x"] = np.ascontiguousarray(xb[:, cid].reshape(TPC, D))
        h = cid
        m["cid"] = np.array([[h * 256, cid * TPC, 0, 0]], dtype=np.int32)
        qi = np.arange(128)[:, None, None] // 64
        rr = np.arange(8)[None, :, None]
        ti = np.arange(128)[None, None, :] // 64
        adm = (rr < cid) | ((rr == cid) & (ti <= qi))
        m["dmask"] = np.ascontiguousarray(np.where(adm, 0.0, NEG).astype(np.float32).reshape(128, 1024))
        murow = np.concatenate([hs(mu[0:512], h), hs(mu[512:1024], h), hs(mu[1024:1536], h), hs(mu[1664:2176], h), mu[1536:1600], mu[1600:1664]])
        m["mu4"] = np.ascontiguousarray(np.broadcast_to(murow[None, None, :], (128, 4, 384)))
        cols = [hs(f(inputs[n])[0].reshape(-1), h) for n in ("w0", "a0", "k_k", "k_a", "r_k", "gn_w", "gn_b")]
        cols.append(np.zeros(64, np.float32))
        m["rwp"] = np.ascontiguousarray(np.stack(cols, axis=1))
        m["wup"] = np.ascontiguousarray(f(inputs["w_up"])[0][:, h * 64:(h + 1) * 64])
        m["aup"] = np.ascontiguousarray(f(inputs["a_up"])[0][:, h * 64:(h + 1) * 64])
        maps.append(m)
    return maps


_CACHE = {}


def kernel(**inputs):
    if "nc" not in _CACHE:
        _CACHE["nc"] = build_program()
    nc, k = _CACHE["nc"]
    maps = make_in_maps(inputs)
    res = run_bass_kernel_spmd(nc, maps, core_ids=list(range(NCORES)))
    out = np.empty((NBL, NCORES, 128, D), np.float32)
    for cid in range(NCORES):
        out[:, cid] = np.asarray(res.results[cid]["out"]).reshape(NBL, 128, D)
    return out.reshape(1, S, D)
```
